# Optimizing a Trainium2 kernel written in Bass

```python
import jax, jax.numpy as jnp
from jax import lax
import numpy as np

D_MODEL = 2048
BATCH = 32
SEQ = 256
DEPTH = 2
DEC_BATCH = 8
DEC_SEQ = 4096
PAST_LEN = 256

GRID_W = 64
N_MIXERS = 2
HEAD_DIM = 128
N_HEADS = D_MODEL // HEAD_DIM
N_KV_HEADS = 4
GQA_GROUP = N_HEADS // N_KV_HEADS
D_ATTN = N_HEADS * HEAD_DIM
D_KV = N_KV_HEADS * HEAD_DIM
QKV_DIM = D_ATTN + 2 * D_KV
ROPE_THETA = 10000.0
ROPE_AXIS_DIM = HEAD_DIM // 2
Q_BLOCK = 128
POOL_WINDOWS = (2, 4, 8, 16)
N_POOL_GROUPS = 4
POOL_GC = D_MODEL // N_POOL_GROUPS
D_FF = 5632
N_ATTN_LAYERS = (DEPTH + 1) // 2
N_POOL_LAYERS = DEPTH // 2
ALPHA = (2.0 * DEPTH) ** 0.25
BETA = (8.0 * DEPTH) ** -0.25
LN_EPS = 1e-5
QK_EPS = 1e-6
MOD_STD = 0.5

kernel_name = 'hybrid_flow_attn_pool_macaron_step'


def layer_norm(x, g, b):
    xf = x.astype(jnp.float32)
    mu = jnp.mean(xf, axis=-1, keepdims=True)
    var = jnp.mean(jnp.square(xf - mu), axis=-1, keepdims=True)
    y = (xf - mu) * lax.rsqrt(var + LN_EPS) * g.astype(jnp.float32) + b.astype(jnp.float32)
    return y.astype(x.dtype)


def rms_norm(x, g):
    xf = x.astype(jnp.float32)
    y = xf * lax.rsqrt(jnp.mean(jnp.square(xf), axis=-1, keepdims=True) + QK_EPS) * g.astype(jnp.float32)
    return y.astype(x.dtype)


def modulation(cond, w_mod, b_mod):
    m = jax.nn.silu(cond) @ w_mod + b_mod
    return m.reshape(cond.shape[0], 3, 3, D_MODEL)


def modulate(x, shift, scale):
    return x * (1.0 + scale[:, None, :]) + shift[:, None, :]


def post_norm_residual(x, y, gate, g, b, weight):
    return layer_norm(ALPHA * x + weight * gate[:, None, :] * y, g, b)


def ffn_sublayer(x, m, w1, w3, w2, g, b):
    h = modulate(x, m[:, 0], m[:, 1])
    f = (jax.nn.silu(h @ w1) * (h @ w3)) @ w2
    return post_norm_residual(x, f, m[:, 2], g, b, 0.5)


def axial_rope_tables(n_tokens):
    rows = n_tokens // GRID_W
    r, cgrid = jnp.meshgrid(jnp.arange(rows), jnp.arange(GRID_W), indexing='ij')
    row = r.reshape(-1).astype(jnp.float32)
    col = cgrid.reshape(-1).astype(jnp.float32)
    half = ROPE_AXIS_DIM // 2
    freqs = ROPE_THETA ** (-jnp.arange(half, dtype=jnp.float32) / half)
    ang_r = row[:, None] * freqs
    ang_c = col[:, None] * freqs
    return jnp.cos(ang_r), jnp.sin(ang_r), jnp.cos(ang_c), jnp.sin(ang_c)


def rope_half(x, cos, sin):
    x1, x2 = jnp.split(x, 2, axis=-1)
    cos = cos[:, None, :]
    sin = sin[:, None, :]
    return jnp.concatenate([x1 * cos - x2 * sin, x1 * sin + x2 * cos], axis=-1)


def apply_axial_rope(x, tables):
    cr, sr, cc, sc = tables
    xf = x.astype(jnp.float32)
    xr, xc = jnp.split(xf, 2, axis=-1)
    out = jnp.concatenate([rope_half(xr, cr, sr), rope_half(xc, cc, sc)], axis=-1)
    return out.astype(x.dtype)


def qkv_heads(h, w_qkv, q_gain, k_gain):
    bsz, n, _ = h.shape
    qkv = h @ w_qkv
    q = qkv[..., :D_ATTN].reshape(bsz, n, N_HEADS, HEAD_DIM)
    k = qkv[..., D_ATTN:D_ATTN + D_KV].reshape(bsz, n, N_KV_HEADS, HEAD_DIM)
    v = qkv[..., D_ATTN + D_KV:].reshape(bsz, n, N_KV_HEADS, HEAD_DIM)
    return rms_norm(q, q_gain), rms_norm(k, k_gain), v


def block_attention(q, k, v):
    bsz, n = q.shape[0], q.shape[1]
    nb = n // Q_BLOCK
    qb = q.reshape(bsz, nb, Q_BLOCK, N_KV_HEADS, GQA_GROUP, HEAD_DIM)
    qb = jnp.moveaxis(qb, 1, 0)
    scale = HEAD_DIM ** -0.5

    def one_block(qblk):
        s = jnp.einsum('bqkgd,blkd->bkgql', qblk, k, preferred_element_type=jnp.float32) * scale
        p = jax.nn.softmax(s, axis=-1).astype(v.dtype)
        return jnp.einsum('bkgql,blkd->bqkgd', p, v)

    o = lax.map(one_block, qb)
    return jnp.moveaxis(o, 0, 1).reshape(bsz, n, D_ATTN)


def pool_mix(h, w_pool, pool_scale):
    bsz, n, _ = h.shape
    hf = h.astype(jnp.float32)
    csum = jnp.concatenate([jnp.zeros((bsz, 1, D_MODEL), jnp.float32), jnp.cumsum(hf, axis=1)], axis=1)
    t = jnp.arange(n)
    outs = []
    for gi, win in enumerate(POOL_WINDOWS):
        lo = jnp.clip(t - win // 2, 0, n)
        hi = jnp.clip(t + win // 2, 0, n)
        cg = csum[..., gi * POOL_GC:(gi + 1) * POOL_GC]
        count = (hi - lo).astype(jnp.float32)[None, :, None]
        mean = (cg[:, hi] - cg[:, lo]) / count
        outs.append(mean - hf[..., gi * POOL_GC:(gi + 1) * POOL_GC])
    d = jnp.stack(outs, axis=2).astype(h.dtype)
    y = jnp.einsum('bngc,gcd->bngd', d, w_pool).reshape(bsz, n, D_MODEL)
    return y * pool_scale


def setup_inputs(seed: int = 0) -> dict:
    key = jax.random.key(seed)
    ks = jax.random.split(key, 20)
    f32 = jnp.float32

    def nrm(k, shape, std):
        return jax.random.normal(k, shape, f32) * std

    return {
        'x_prompt': nrm(ks[0], (BATCH, SEQ, D_MODEL), 1.0),
        'x_sample': nrm(ks[1], (DEC_BATCH, DEC_SEQ, D_MODEL), 1.0),
        'cache_k': nrm(ks[2], (DEC_BATCH, N_ATTN_LAYERS, PAST_LEN, N_KV_HEADS, HEAD_DIM), 1.0),
        'cache_v': nrm(ks[3], (DEC_BATCH, N_ATTN_LAYERS, PAST_LEN, N_KV_HEADS, HEAD_DIM), 1.0),
        'c': nrm(ks[4], (DEC_BATCH, D_MODEL), 1.0),
        'c_ctx': nrm(ks[5], (D_MODEL,), 1.0),
        'w_mod': nrm(ks[6], (DEPTH, D_MODEL, 9 * D_MODEL), MOD_STD * D_MODEL ** -0.5),
        'b_mod': nrm(ks[7], (DEPTH, 9 * D_MODEL), 0.02),
        'ln_g': 1.0 + nrm(ks[8], (DEPTH, 3, D_MODEL), 0.02),
        'ln_b': nrm(ks[9], (DEPTH, 3, D_MODEL), 0.02),
        'ffn_w1': nrm(ks[10], (DEPTH, 2, D_MODEL, D_FF), D_MODEL ** -0.5),
        'ffn_w3': nrm(ks[11], (DEPTH, 2, D_MODEL, D_FF), D_MODEL ** -0.5),
        'ffn_w2': nrm(ks[12], (DEPTH, 2, D_FF, D_MODEL), BETA * D_FF ** -0.5),
        'w_qkv': nrm(ks[13], (N_ATTN_LAYERS, D_MODEL, QKV_DIM), D_MODEL ** -0.5),
        'q_gain': 1.0 + nrm(ks[14], (N_ATTN_LAYERS, HEAD_DIM), 0.02),
        'k_gain': 1.0 + nrm(ks[15], (N_ATTN_LAYERS, HEAD_DIM), 0.02),
        'w_o': nrm(ks[16], (N_ATTN_LAYERS, D_ATTN, D_MODEL), BETA * D_ATTN ** -0.5),
        'w_pool': nrm(ks[17], (N_POOL_LAYERS, N_POOL_GROUPS, POOL_GC, POOL_GC), BETA * POOL_GC ** -0.5),
        'pool_scale': 1.0 + nrm(ks[18], (N_POOL_LAYERS, D_MODEL), 0.1),
    }


def reference(x_prompt, x_sample, cache_k, cache_v, c, c_ctx, w_mod, b_mod, ln_g, ln_b,
              ffn_w1, ffn_w3, ffn_w2, w_qkv, q_gain, k_gain, w_o, w_pool, pool_scale):
    x = x_prompt
    new_k = []
    new_v = []
    for i in range(DEPTH):
        m = modulation(c_ctx[None, :], w_mod[i], b_mod[i])
        x = ffn_sublayer(x, m[:, 0], ffn_w1[i, 0], ffn_w3[i, 0], ffn_w2[i, 0], ln_g[i, 0], ln_b[i, 0])
        h = modulate(x, m[:, 1, 0], m[:, 1, 1])
        j = i // N_MIXERS
        if i % N_MIXERS == 0:
            q, k, v = qkv_heads(h, w_qkv[j], q_gain[j], k_gain[j])
            y = block_attention(q, k, v) @ w_o[j]
            new_k.append(k)
            new_v.append(v)
        else:
            y = pool_mix(h, w_pool[j], pool_scale[j])
        x = post_norm_residual(x, y, m[:, 1, 2], ln_g[i, 1], ln_b[i, 1], 1.0)
        x = ffn_sublayer(x, m[:, 2], ffn_w1[i, 1], ffn_w3[i, 1], ffn_w2[i, 1], ln_g[i, 2], ln_b[i, 2])
    y_prompt = x
    ctx_k = jnp.stack(new_k, axis=1)
    ctx_v = jnp.stack(new_v, axis=1)

    rope = axial_rope_tables(x_sample.shape[1])
    x = x_sample
    for i in range(DEPTH):
        m = modulation(c, w_mod[i], b_mod[i])
        x = ffn_sublayer(x, m[:, 0], ffn_w1[i, 0], ffn_w3[i, 0], ffn_w2[i, 0], ln_g[i, 0], ln_b[i, 0])
        h = modulate(x, m[:, 1, 0], m[:, 1, 1])
        j = i // N_MIXERS
        if i % N_MIXERS == 0:
            q, k, v = qkv_heads(h, w_qkv[j], q_gain[j], k_gain[j])
            q = apply_axial_rope(q, rope)
            k = apply_axial_rope(k, rope)
            k_all = jnp.concatenate([cache_k[:, j].astype(k.dtype), k], axis=1)
            v_all = jnp.concatenate([cache_v[:, j].astype(v.dtype), v], axis=1)
            y = block_attention(q, k_all, v_all) @ w_o[j]
        else:
            y = pool_mix(h, w_pool[j], pool_scale[j])
        x = post_norm_residual(x, y, m[:, 1, 2], ln_g[i, 1], ln_b[i, 1], 1.0)
        x = ffn_sublayer(x, m[:, 2], ffn_w1[i, 1], ffn_w3[i, 1], ffn_w2[i, 1], ln_g[i, 2], ln_b[i, 2])
    y_sample = x
    return (y_prompt, y_sample, ctx_k, ctx_v)
```

```python
import os
import numpy as np
from contextlib import ExitStack
import concourse.bass as bass
import concourse.mybir as mybir
from concourse.bass_utils import run_bass_kernel_spmd

F32 = mybir.dt.float32
BF16 = mybir.dt.bfloat16
AF = mybir.ActivationFunctionType
ALU = mybir.AluOpType
AX = mybir.AxisListType

D = 2048
DC = 16
DFF = 5632
FC = 44
T = 512
NT_S = int(os.environ.get("MK_NTS", "8"))
NT_P = int(os.environ.get("MK_NTP", "2"))
NCORES = int(os.environ.get("MK_CORES", "8"))
NT = NT_S + NT_P
NTOK = NT * T
HD = 128
NH = 16
NKV = 4
PAST = 256
NKC = (PAST + NT_S * T) // 128
ALPHA = 4.0 ** 0.25
EPS_LN = 1e-5 / (ALPHA * ALPHA)
EPS_QK = 1e-6
SM_SCALE = HD ** -0.5
POOL_WINDOWS = (2, 4, 8, 16)

STAGE = int(os.environ.get("MK_STAGE", "99"))


class Buf:
    __slots__ = ("name", "w", "r_eng", "r_dma")

    def __init__(self, name):
        self.name = name
        self.w = None
        self.r_eng = {}
        self.r_dma = []


class Op:
    __slots__ = ("eng", "fn", "is_dma", "deps", "signal", "ticket", "sem", "val", "idx")

    def __init__(self, eng, fn, is_dma):
        self.eng = eng
        self.fn = fn
        self.is_dma = is_dma
        self.deps = []
        self.signal = False
        self.ticket = 0
        self.sem = None
        self.val = 0


class Ring:
    def __init__(self, name, sems):
        self.name = name
        self.sems = sems
        self.ops = []


class Prog:
    ENGS = ("pe", "act", "dve", "pool", "sp")

    def __init__(self):
        self.ops = []
        self.rings = []
        self.last = {}

    def ring(self, name, sems):
        r = Ring(name, sems)
        self.rings.append(r)
        return r

    def add(self, eng, fn, reads=(), writes=(), ring=None, extra=()):
        is_dma = ring is not None
        op = Op(eng, fn, is_dma)
        op.idx = len(self.ops)
        deps = {}

        def need(d, raw):
            if d is None:
                return
            if (not d.is_dma) and (not is_dma) and d.eng == eng:
                if eng == "pe" or not raw:
                    return
            deps[id(d)] = d

        for b in reads:
            need(b.w, True)
        for b in writes:
            need(b.w, False)
            for d in b.r_eng.values():
                need(d, False)
            for d in b.r_dma:
                need(d, False)
        for d in extra:
            deps[id(d)] = d
        if is_dma:
            k = len(ring.ops)
            R = len(ring.sems)
            if k >= R:
                d = ring.ops[k - R]
                deps[id(d)] = d
            op.sem = ring.sems[k % R]
            op.val = 16 * (k // R + 1)
            ring.ops.append(op)
        for d in deps.values():
            d.signal = True
        op.deps = list(deps.values())
        for b in reads:
            if is_dma:
                b.r_dma.append(op)
            else:
                b.r_eng[eng] = op
        for b in writes:
            b.w = op
            b.r_eng = {}
            b.r_dma = []
        self.ops.append(op)
        if not is_dma:
            self.last[eng] = op
        return op

    def pe(self, fn, reads=(), writes=()):
        return self.add("pe", fn, reads, writes)

    def act(self, fn, reads=(), writes=()):
        return self.add("act", fn, reads, writes)

    def dve(self, fn, reads=(), writes=()):
        return self.add("dve", fn, reads, writes)

    def pool(self, fn, reads=(), writes=()):
        return self.add("pool", fn, reads, writes)

    def dma(self, queue, ring, out, in_, reads=(), writes=(), **kw):
        return self.add(queue, lambda e: e.dma_start(out=out, in_=in_, **kw), reads, writes, ring=ring)

    def barrier(self, skip=()):
        ds = [op for op in self.last.values()]
        for r in self.rings:
            if r in skip:
                continue
            ds.extend(r.ops[-len(r.sems):])
        for eng in self.ENGS:
            self.add(eng, None, extra=[d for d in ds if not (d.eng == eng and not d.is_dma)])
        self.last = {}

    def emit(self, eng, e, sems):
        known = {}
        for op in self.ops:
            if op.eng != eng:
                continue
            for d in op.deps:
                if d.is_dma:
                    s, v = d.sem, d.val
                else:
                    s, v = sems[d.eng], d.ticket
                key = id(s)
                if known.get(key, 0) >= v:
                    continue
                known[key] = v
                e.wait_ge(s, v)
            if op.fn is None:
                continue
            inst = op.fn(e)
            if op.is_dma:
                inst.then_inc(op.sem, 16)
            elif op.signal:
                inst.then_inc(sems[eng], 1)

    def assign_tickets(self):
        cnt = {}
        for op in self.ops:
            if op.is_dma or op.fn is None:
                continue
            if op.signal:
                cnt[op.eng] = cnt.get(op.eng, 0) + 1
                op.ticket = cnt[op.eng]
        return cnt


def build_nc():
    nc = bass.Bass("TRN2", target_bir_lowering=False)
    P = Prog()

    def din(name, shape, dt=F32):
        return nc.dram_tensor(name, list(shape), dt, kind="ExternalInput").ap()

    def dout(name, shape, dt=F32):
        return nc.dram_tensor(name, list(shape), dt, kind="ExternalOutput").ap()

    def dscr(name, shape, dt=F32):
        return nc.dram_tensor(name, list(shape), dt, kind="Internal").ap()

    xs = din("xs", [4096, D])
    xp = din("xp", [1024, D])
    ck = din("ck", [PAST, 512])
    cv = din("cv", [PAST, 512])
    cc = din("cc", [32, 128])
    w_mod = din("w_mod", [2, D, 9 * D])
    b_mod = din("b_mod", [288, 128])
    ln_g = din("ln_g", [96, 128])
    ln_b = din("ln_b", [96, 128])
    w1 = din("ffn_w1", [4, D, DFF])
    w3 = din("ffn_w3", [4, D, DFF])
    w2 = din("ffn_w2", [4, DFF, D])
    w_qkv = din("w_qkv", [D, 3072])
    q_gain = din("q_gain", [1, 128])
    k_gain = din("k_gain", [1, 128])
    w_o = din("w_o", [D, D])
    w_pool = din("w_pool", [4, 512, 512])
    pool_scale = din("pool_scale", [16, 128])
    ropeR = din("ropeR", [32, 128, 256])

    ys = dout("ys", [4096, D])
    yp = dout("yp", [1024, D])
    okk = dout("ok", [1024, 512])
    ovv = dout("ov", [1024, 512])

    zs = [dscr("zA", [D, 5120]), dscr("zB", [D, 5120])]
    sts = [dscr("stA", [2, 5120]), dscr("stB", [2, 5120])]
    w13s = dscr("w13s", [4, FC, 128, 2, 16, 128], BF16)
    w2s = dscr("w2s", [4, DC, 128, FC, 128], BF16)
    wqs = dscr("wqs", [6, 128, 16, 512], BF16)
    wos = dscr("wos", [DC, 128, 16, 128], BF16)

    zbuf = [[[Buf(f"z{a}_{t}_{d}") for d in range(DC)] for t in range(NT)] for a in range(2)]
    stbuf = [[[Buf(f"st{a}_{t}_{k}") for k in range(2)] for t in range(NT)] for a in range(2)]
    w13b = [[[Buf(f"w13_{m}_{f}_{k}") for k in range(2)] for f in range(FC)] for m in range(4)]
    w2b = [[[Buf(f"w2_{m}_{d}_{k}") for k in range(2)] for d in range(DC)] for m in range(4)]
    wqb = [Buf(f"wq_{c}") for c in range(6)]
    wob = [Buf(f"wo_{d}") for d in range(DC)]

    tiles = [(512 * i, 0) for i in range(NT_S)] + [(4096 + 512 * i, 1) for i in range(NT_P)]
    assert NTOK == NT * T

    with ExitStack() as gs:
        uniq = [0]

        def sb(name, shape, dt=F32, st=gs):
            uniq[0] += 1
            return st.enter_context(nc.sbuf_tensor(f"{name}_u{uniq[0]}", list(shape), dt))

        def sem(name):
            return gs.enter_context(nc.semaphore(name))

        esem = {e: sem("s_" + e) for e in ("pe", "act", "dve", "pool")}

        def mkring(name, n):
            return P.ring(name, [sem(f"r_{name}{i}") for i in range(n)])

        r_w = mkring("w", 4)
        r_zl = mkring("zl", 4)
        r_zs = mkring("zs", 4)
        r_cv = mkring("cv", 8)
        r_st = mkring("st", 4)
        r_m = mkring("m", 4)
        r_x = mkring("x", 2)
        r_o = mkring("o", 4)

        ps = [gs.enter_context(nc.psum_tensor(f"ps{i}", [128, 512], F32)) for i in range(8)]
        psb = [Buf(f"ps{i}") for i in range(8)]

        ident_f = sb("ident_f", [128, 128]); ident_b = sb("ident_b", [128, 128], BF16)
        ones_f = sb("ones_f", [128, 128]); ones_b = sb("ones_b", [128, 128], BF16)
        epsl = sb("epsl", [128, 1]); epsq = sb("epsq", [128, 1]); mhalf = sb("mhalf", [128, 4])
        cbuf = Buf("consts")
        itmp = sb("itmp", [128, 128])
        P.pool(lambda e: e.memset(ident_f[:], 0.0), writes=[cbuf])
        P.pool(lambda e: e.iota(itmp[:], pattern=[[1, 128]], base=0, channel_multiplier=-1,
                                allow_small_or_imprecise_dtypes=True), writes=[cbuf])
        P.pool(lambda e: e.tensor_scalar(out=ident_f[:], in0=itmp[:], scalar1=0.0, scalar2=None, op0=ALU.is_equal),
               reads=[cbuf], writes=[cbuf])
        P.pool(lambda e: e.tensor_copy(out=ident_b[:], in_=ident_f[:]), reads=[cbuf], writes=[cbuf])
        P.pool(lambda e: e.memset(ones_f[:], 1.0), writes=[cbuf])
        P.pool(lambda e: e.memset(ones_b[:], 1.0), writes=[cbuf])
        P.pool(lambda e: e.memset(epsl[:], EPS_LN), writes=[cbuf])
        P.pool(lambda e: e.memset(epsq[:], EPS_QK), writes=[cbuf])
        P.pool(lambda e: e.memset(mhalf[:], -0.5), writes=[cbuf])

        modT = sb("modT", [128, 2, 2, 9, 16])
        lngT = sb("lngT", [128, 96]); lnbT = sb("lnbT", [128, 96])
        pscT = sb("pscT", [128, 16])
        qgain4 = sb("qgain4", [128, 128]); kgain4 = sb("kgain4", [128, 128])
        modb = [[Buf(f"modT{l}{s_}") for s_ in range(3)] for l in range(2)]
        cndb = Buf("cond")

        def conv_w13_pair(m, f):
            for wi, wsrc in enumerate((w1, w3)):
                P.dma("pool", r_cv, w13s[m, f, :, wi, :, :],
                      wsrc[m, :, f * 128:(f + 1) * 128].rearrange("(kc p) n -> p kc n", p=128),
                      writes=[w13b[m][f][wi]])

        def conv_w2(m, d):
            for half in range(2):
                f0, f1 = half * 22, half * 22 + 22
                P.dma("pool", r_cv, w2s[m, d, :, f0:f1, :],
                      w2[m, f0 * 128:f1 * 128, d * 128:(d + 1) * 128].rearrange("(fc p) n -> p fc n", p=128),
                      writes=[w2b[m][d][half]])

        def conv_wq(c):
            P.dma("pool", r_cv, wqs[c], w_qkv[:, c * 512:(c + 1) * 512].rearrange("(kc p) n -> p kc n", p=128),
                  writes=[wqb[c]])

        def conv_wo(d):
            P.dma("pool", r_cv, wos[d], w_o[:, d * 128:(d + 1) * 128].rearrange("(kc p) n -> p kc n", p=128),
                  writes=[wob[d]])

        conv_list = []
        def queue_ffn_conv(m):
            for f in range(FC):
                conv_list.append(lambda m=m, f=f: conv_w13_pair(m, f))
            for d in range(DC):
                conv_list.append(lambda m=m, d=d: conv_w2(m, d))
        conv_pos = [0]

        def conv_step(n):
            for _ in range(n):
                if conv_pos[0] < len(conv_list):
                    conv_list[conv_pos[0]]()
                    conv_pos[0] += 1

        def conv_until(target):
            while conv_pos[0] < min(target, len(conv_list)):
                conv_list[conv_pos[0]]()
                conv_pos[0] += 1

        queue_ffn_conv(0)
        mark_ffn0a = len(conv_list)
        for c in (4, 5):
            conv_list.append(lambda c=c: conv_wq(c))
        for c in range(4):
            conv_list.append(lambda c=c: conv_wq(c))
        for d in range(DC):
            conv_list.append(lambda d=d: conv_wo(d))
        mark_att = len(conv_list)
        queue_ffn_conv(1)
        mark_ffn0b = len(conv_list)
        queue_ffn_conv(2)
        mark_ffn1a = len(conv_list)
        queue_ffn_conv(3)
        mark_ffn1b = len(conv_list)

        condT = sb("condT", [128, 32], BF16)
        bmT = sb("bmT", [128, 288])
        r_wm = mkring("wm", 2)
        cT = condT[:].rearrange("p (r k) -> p r k", r=2)
        mod_items = [(l, 3 * s_ + k, b_) for l in range(2) for s_ in range(3) for k in range(3) for b_ in range(8)]
        mod_pos = [0]

        mod_dma = [0]

        mod_limit = [24]

        def mod_issue(wm, wmb):
            k = mod_dma[0]
            if k >= min(len(mod_items), mod_limit[0]):
                return
            l, j, b_ = mod_items[k]
            sl_ = k % 2
            mod_dma[0] += 1
            c0 = j * D + b_ * 256
            P.dma("pool", r_wm, wm[sl_][:], w_mod[l, :, c0:c0 + 256].rearrange("(kc p) n -> p kc n", p=128), writes=[wmb[sl_]])

        def mod_step(n, wm, wmb, bank, tail_prefetch=True):
            for it_ in range(n):
                if mod_pos[0] >= len(mod_items):
                    return
                if mod_dma[0] <= mod_pos[0]:
                    mod_issue(wm, wmb)
                l, j, b_ = mod_items[mod_pos[0]]
                sl_ = mod_pos[0] % 2
                mod_pos[0] += 1
                for q in range(2):
                    for kc in range(16):
                        P.pe(lambda e, sl_=sl_, q=q, kc=kc: e.matmul(ps[bank][:, q * 2:q * 2 + 2], wm[sl_][:, kc, q * 128:(q + 1) * 128], cT[:, :, kc],
                                                                 start=(kc == 0), stop=(kc == 15)),
                             reads=[wmb[sl_], cndb], writes=[psb[bank]])
                o = (l * 9 + j) * 16 + 2 * b_
                for r in range(2):
                    P.dve(lambda e, l=l, j=j, b_=b_, r=r, o=o: e.tensor_tensor(
                        out=modT[:, r, l, j, 2 * b_:2 * b_ + 2],
                        in0=ps[bank][:, 0:4].rearrange("p (c r) -> p c r", r=2)[:, :, r],
                        in1=bmT[:, o:o + 2], op=ALU.add),
                        reads=[psb[bank], cndb], writes=[modb[l][j // 3]])
                if tail_prefetch or it_ < n - 1:
                    mod_issue(wm, wmb)

        ld = sb("ld_small", [128, 6, 128])
        ldb = Buf("ld_small")
        ld2 = sb("ld_small2", [128, 128])
        ld2b = Buf("ld2")
        sig = sb("sigc", [128, 128])
        if True:
            P.pool(lambda e: e.memset(ld[:], 0.0), writes=[ldb])
            P.dma("sp", r_m, ld[0:32, 0, :], cc[:, :], writes=[ldb])
            P.dma("sp", r_m, ld[0:96, 1, :], b_mod[0:96, :], writes=[ldb])
            P.dma("sp", r_m, ld[0:96, 2, :], b_mod[96:192, :], writes=[ldb])
            P.dma("sp", r_m, ld[0:96, 3, :], b_mod[192:288, :], writes=[ldb])
            P.dma("sp", r_m, ld[0:96, 4, :], ln_g[:, :], writes=[ldb])
            P.dma("sp", r_m, ld[0:96, 5, :], ln_b[:, :], writes=[ldb])
            P.pool(lambda e: e.memset(ld2[:], 0.0), writes=[ld2b])
            P.dma("sp", r_m, ld2[0:16, :], pool_scale[:, :], writes=[ld2b])
            P.dma("sp", r_m, qgain4[:], q_gain[0, :].partition_broadcast(128), writes=[cbuf])
            P.dma("sp", r_m, kgain4[:], k_gain[0, :].partition_broadcast(128), writes=[cbuf])
            P.act(lambda e: e.activation(out=sig[:], in_=ld[:, 0, :], func=AF.Silu), reads=[ldb], writes=[ldb])
            P.pe(lambda e: e.transpose(ps[0][:, 0:128], sig[:], ident_f[:]), reads=[ldb, cbuf], writes=[psb[0]])
            for i in range(1, 6):
                P.pe(lambda e, i=i: e.transpose(ps[i // 4][:, (i % 4) * 128:(i % 4 + 1) * 128], ld[:, i, :], ident_f[:]),
                     reads=[ldb, cbuf], writes=[psb[i // 4]])
            P.pe(lambda e: e.transpose(ps[1][:, 256:384], ld2[:], ident_f[:]), reads=[ld2b, cbuf], writes=[psb[1]])
            P.dve(lambda e: e.tensor_copy(out=condT[:], in_=ps[0][:, 0:32]), reads=[psb[0]], writes=[cndb])
            for i in range(3):
                P.dve(lambda e, i=i: e.tensor_copy(out=bmT[:, i * 96:(i + 1) * 96], in_=ps[0][:, (i + 1) * 128:(i + 1) * 128 + 96]),
                      reads=[psb[0]], writes=[cndb])
            P.dve(lambda e: e.tensor_copy(out=lngT[:], in_=ps[1][:, 0:96]), reads=[psb[1]], writes=[cbuf])
            P.dve(lambda e: e.tensor_copy(out=lnbT[:], in_=ps[1][:, 128:224]), reads=[psb[1]], writes=[cbuf])
            P.dve(lambda e: e.tensor_copy(out=pscT[:], in_=ps[1][:, 256:272]), reads=[psb[1]], writes=[cbuf])
        with ExitStack() as s0:
            wm0 = [sb(f"wm{i}", [128, 16, 256], BF16, st=s0) for i in range(2)]
            wm0b = [Buf(f"wm{i}") for i in range(2)]
            mod_step(24, wm0, wm0b, 2, tail_prefetch=False)
            conv_until(mark_ffn0a)
            P.barrier(skip=[r_cv])

        NZ = 4
        zst = [sb(f"zst{i}", [128, T]) for i in range(NZ)]; zstb = [Buf(f"zst{i}") for i in range(NZ)]
        ntm = [sb(f"ntm{i}", [128, T]) for i in range(2)]; ntmb = [Buf(f"ntm{i}") for i in range(2)]
        xtm_ = [sb(f"xe{i}", [128, T]) for i in range(2)]; xtmb = [Buf(f"xe{i}") for i in range(2)]
        zn = [sb(f"zn{i}", [128, T]) for i in range(3)]; znb = [Buf(f"zn{i}") for i in range(3)]
        sqt = [sb(f"sq{i}", [128, T]) for i in range(2)]; sqb = [Buf(f"sq{i}") for i in range(2)]
        S1 = sb("S1", [128, T]); S2 = sb("S2", [128, T]); S1b = Buf("S1"); S2b = Buf("S2")
        stat_in = [[sb(f"sti{i}_{k}", [128, T]) for k in range(2)] for i in range(2)]
        stat_inb = [[Buf(f"sti{i}_{k}") for k in range(2)] for i in range(2)]
        mean_t = sb("mean_t", [128, T]); rstd_t = sb("rstd_t", [128, T]); nmr_t = sb("nmr_t", [128, T])
        msq_t = rstd_t; var_t = nmr_t
        finb = Buf("fin")
        tab = sb("tab", [128, 2, 3, 16])
        tabb = Buf("tab")
        cnt = {"zst": 0, "ntm": 0, "xe": 0, "zn": 0, "sq": 0, "sti": 0}

        def nxt(k, n):
            v = cnt[k] % n
            cnt[k] += 1
            return v

        def make_tables(l, s, prev, wgt, use_pool_scale=False):
            for r in range(2):
                sh = modT[:, r, l, 3 * s + 0, :]
                sc = modT[:, r, l, 3 * s + 1, :]
                gt = modT[:, r, l, 3 * s + 2, :]
                if prev is None:
                    P.dve(lambda e, r=r, sc=sc: e.tensor_scalar(out=tab[:, r, 0, :], in0=sc, scalar1=1.0, scalar2=None, op0=ALU.add),
                          reads=[modb[l][s]], writes=[tabb])
                    P.dve(lambda e, r=r, sh=sh: e.tensor_copy(out=tab[:, r, 1, :], in_=sh), reads=[modb[l][s]], writes=[tabb])
                else:
                    o = (prev[0] * 3 + prev[1]) * 16
                    G = lngT[:, o:o + 16]
                    Bv = lnbT[:, o:o + 16]
                    P.dve(lambda e, r=r, sc=sc, G=G: e.scalar_tensor_tensor(out=tab[:, r, 0, :], in0=sc, scalar=1.0, in1=G,
                                                                        op0=ALU.add, op1=ALU.mult),
                          reads=[modb[l][s], cbuf], writes=[tabb])
                    P.dve(lambda e, r=r, sc=sc, Bv=Bv: e.scalar_tensor_tensor(out=tab[:, r, 1, :], in0=sc, scalar=1.0, in1=Bv,
                                                                          op0=ALU.add, op1=ALU.mult),
                          reads=[modb[l][s], cbuf], writes=[tabb])
                    P.dve(lambda e, r=r, sh=sh: e.tensor_tensor(out=tab[:, r, 1, :], in0=tab[:, r, 1, :], in1=sh, op=ALU.add),
                          reads=[modb[l][s], tabb], writes=[tabb])
                if use_pool_scale:
                    P.dve(lambda e, r=r, gt=gt: e.scalar_tensor_tensor(out=tab[:, r, 2, :], in0=gt, scalar=wgt / ALPHA, in1=pscT[:],
                                                                   op0=ALU.mult, op1=ALU.mult),
                          reads=[modb[l][s], cbuf], writes=[tabb])
                else:
                    P.dve(lambda e, r=r, gt=gt: e.tensor_scalar(out=tab[:, r, 2, :], in0=gt, scalar1=wgt / ALPHA, scalar2=None,
                                                            op0=ALU.mult),
                          reads=[modb[l][s]], writes=[tabb])

        def load_stats(src, ti):
            tok0 = tiles[ti][0]
            s = nxt("sti", 2)
            for k in range(2):
                P.dma("sp", r_st, stat_in[s][k][:], sts[src][k, tok0:tok0 + T].partition_broadcast(128),
                      reads=[stbuf[src][ti][k]], writes=[stat_inb[s][k]])
            return s

        norm_on_dve = [False]

        def load_norm(src, ti, dc, stslot, c0=0, c1=T, tcol=None):
            tok0 = tiles[ti][0]
            zi = nxt("zst", NZ)
            P.dma("sp", r_zl, zst[zi][:, c0:c1], zs[src][dc * 128:(dc + 1) * 128, tok0 + c0:tok0 + c1],
                  reads=[zbuf[src][ti][dc]], writes=[zstb[zi]])
            if stslot is None:
                return zst[zi], zstb[zi]
            ni = nxt("ntm", 2)
            (P.dve if norm_on_dve[0] else P.pool)(
                lambda e: e.tensor_tensor(out=zst[zi][:, c0:c1], in0=zst[zi][:, c0:c1], in1=stat_in[stslot][0][:, c0:c1], op=ALU.mult),
                reads=[zstb[zi], stat_inb[stslot][0]], writes=[zstb[zi]])
            P.dve(lambda e: e.tensor_tensor(out=ntm[ni][:, c0:c1], in0=zst[zi][:, c0:c1], in1=stat_in[stslot][1][:, c0:c1], op=ALU.add),
                  reads=[zstb[zi], stat_inb[stslot][1]], writes=[ntmb[ni]])
            return ntm[ni], ntmb[ni]

        def prologue_dc(src, ti, stslot, h_t, hb, dc):
            r = tiles[ti][1]
            n_ap, n_b = load_norm(src, ti, dc, stslot)
            P.act(lambda e, dc=dc, n_ap=n_ap: e.activation(out=h_t[:, dc, :], in_=n_ap[:], func=AF.Identity,
                                                         scale=tab[:, r, 0, dc:dc + 1], bias=tab[:, r, 1, dc:dc + 1]),
                  reads=[n_b, tabb], writes=[hb])

        def prologue(src, ti, stslot, h_t, hb):
            for dc in range(DC):
                prologue_dc(src, ti, stslot, h_t, hb, dc)

        def epi_begin():
            pass

        def epilogue(src, dst, ti, dc, stslot, prev, ybank, first, s_eng=None, pe_stats=None):
            tok0, r = tiles[ti]
            n_ap, n_b = load_norm(src, ti, dc, stslot)
            if prev is not None:
                xi = nxt("xe", 2)
                o = (prev[0] * 3 + prev[1]) * 16 + dc
                P.act(lambda e: e.activation(out=xtm_[xi][:], in_=n_ap[:], func=AF.Identity,
                                             scale=lngT[:, o:o + 1], bias=lnbT[:, o:o + 1]),
                      reads=[n_b, cbuf], writes=[xtmb[xi]])
                x_ap, x_b = xtm_[xi], xtmb[xi]
            else:
                x_ap, x_b = n_ap, n_b
            zi = nxt("zn", 3)
            P.dve(lambda e: e.scalar_tensor_tensor(out=zn[zi][:], in0=ps[ybank][:], scalar=tab[:, r, 2, dc:dc + 1], in1=x_ap[:],
                                                   op0=ALU.mult, op1=ALU.add),
                  reads=[psb[ybank], x_b, tabb], writes=[znb[zi]])
            si = nxt("sq", 2)
            P.act(lambda e: e.activation(out=sqt[si][:], in_=zn[zi][:], func=AF.Square), reads=[znb[zi]], writes=[sqb[si]])
            se = s_eng or P.pool
            if pe_stats is not None:
                b1, b2, last = pe_stats
                P.pe(lambda e: e.matmul(ps[b1][:], ones_f[:], zn[zi][:], start=first, stop=last), reads=[znb[zi], cbuf], writes=[psb[b1]])
                P.pe(lambda e: e.matmul(ps[b2][:], ones_f[:], sqt[si][:], start=first, stop=last), reads=[sqb[si], cbuf], writes=[psb[b2]])
            elif first:
                se(lambda e: e.tensor_copy(out=S1[:], in_=zn[zi][:]), reads=[znb[zi]], writes=[S1b])
                se(lambda e: e.tensor_copy(out=S2[:], in_=sqt[si][:]), reads=[sqb[si]], writes=[S2b])
            else:
                se(lambda e: e.tensor_tensor(out=S1[:], in0=S1[:], in1=zn[zi][:], op=ALU.add), reads=[znb[zi], S1b], writes=[S1b])
                se(lambda e: e.tensor_tensor(out=S2[:], in0=S2[:], in1=sqt[si][:], op=ALU.add), reads=[sqb[si], S2b], writes=[S2b])
            P.dma("act", r_zs, zs[dst][dc * 128:(dc + 1) * 128, tok0:tok0 + T], zn[zi][:],
                  reads=[znb[zi]], writes=[zbuf[dst][ti][dc]])

        def finalize(dst, ti, b1=6, b2=7, have_tot=False):
            tok0 = tiles[ti][0]
            if not have_tot:
                P.pe(lambda e: e.matmul(ps[b1][:], ones_f[:], S1[:], start=True, stop=True), reads=[S1b, cbuf], writes=[psb[b1]])
                P.pe(lambda e: e.matmul(ps[b2][:], ones_f[:], S2[:], start=True, stop=True), reads=[S2b, cbuf], writes=[psb[b2]])
            P.act(lambda e: e.activation(out=mean_t[:], in_=ps[b1][:], func=AF.Copy, scale=1.0 / D), reads=[psb[b1]], writes=[finb])
            P.dve(lambda e: e.tensor_tensor(out=msq_t[:], in0=mean_t[:], in1=mean_t[:], op=ALU.mult), reads=[finb], writes=[finb])
            P.dve(lambda e: e.scalar_tensor_tensor(out=var_t[:], in0=ps[b2][:], scalar=1.0 / D, in1=msq_t[:],
                                                   op0=ALU.mult, op1=ALU.subtract), reads=[psb[b2], finb], writes=[finb])
            P.act(lambda e: e.activation(out=var_t[:], in_=var_t[:], func=AF.Sqrt, bias=epsl[:, 0:1], scale=1.0),
                  reads=[finb, cbuf], writes=[finb])
            P.dve(lambda e: e.reciprocal(out=rstd_t[:], in_=var_t[:]), reads=[finb], writes=[finb])
            P.dve(lambda e: e.scalar_tensor_tensor(out=nmr_t[:], in0=mean_t[:], scalar=-1.0, in1=rstd_t[:],
                                                   op0=ALU.mult, op1=ALU.mult), reads=[finb], writes=[finb])
            P.dma("sp", r_st, sts[dst][0:1, tok0:tok0 + T], rstd_t[0:1, :], reads=[finb], writes=[stbuf[dst][ti][0]])
            P.dma("sp", r_st, sts[dst][1:2, tok0:tok0 + T], nmr_t[0:1, :], reads=[finb], writes=[stbuf[dst][ti][1]])

        def pass_in(dst):
            with ExitStack() as st:
                xin = [sb(f"xin{i}", [128, 4, D], st=st) for i in range(2)]
                xinb = [Buf(f"xin{i}") for i in range(2)]
                k = 0
                for ti in range(NT):
                    tok0, r = tiles[ti]
                    s = ti % 2
                    srcap = xs[tok0:tok0 + T, :] if r == 0 else xp[tok0 - 4096:tok0 - 4096 + T, :]
                    P.dma("sp", r_x, xin[s][:], srcap.rearrange("(tb p) d -> p tb d", p=128), writes=[xinb[s]])
                    for dc in range(DC):
                        bank = 4 + (k % 4)
                        for tb in range(4):
                            P.pe(lambda e, s=s, tb=tb, dc=dc, bank=bank: e.transpose(
                                ps[bank][:, tb * 128:(tb + 1) * 128], xin[s][:, tb, dc * 128:(dc + 1) * 128], ident_f[:]),
                                reads=[xinb[s], cbuf], writes=[psb[bank]])
                        zi = nxt("zn", 3)
                        if k % 2 == 0:
                            P.dve(lambda e, zi=zi, bank=bank: e.tensor_copy(out=zn[zi][:], in_=ps[bank][:]), reads=[psb[bank]], writes=[znb[zi]])
                        else:
                            P.act(lambda e, zi=zi, bank=bank: e.activation(out=zn[zi][:], in_=ps[bank][:], func=AF.Copy), reads=[psb[bank]], writes=[znb[zi]])
                        P.dma("act", r_zs, zs[dst][dc * 128:(dc + 1) * 128, tok0:tok0 + T], zn[zi][:],
                              reads=[znb[zi]], writes=[zbuf[dst][ti][dc]])
                        k += 1
                P.barrier(skip=[r_cv])

        def pass_out(src, prev):
            with ExitStack() as st:
                xf = [sb(f"xf{i}", [128, DC, T], st=st) for i in range(2)]
                xfb = [Buf(f"xf{i}") for i in range(2)]
                yt = [sb(f"yt{i}", [128, D], st=st) for i in range(2)]
                ytb = [Buf(f"yt{i}") for i in range(2)]
                k = 0
                for ti in range(NT):
                    tok0, r = tiles[ti]
                    s = ti % 2
                    stslot = load_stats(src, ti) if prev is not None else None
                    for dc in range(DC):
                        n_ap, n_b = load_norm(src, ti, dc, stslot)
                        if prev is not None:
                            o = (prev[0] * 3 + prev[1]) * 16 + dc
                            P.act(lambda e, dc=dc, n_ap=n_ap, o=o, s=s: e.activation(
                                out=xf[s][:, dc, :], in_=n_ap[:], func=AF.Identity, scale=lngT[:, o:o + 1], bias=lnbT[:, o:o + 1]),
                                reads=[n_b, cbuf], writes=[xfb[s]])
                        else:
                            P.act(lambda e, dc=dc, n_ap=n_ap, s=s: e.activation(out=xf[s][:, dc, :], in_=n_ap[:], func=AF.Copy),
                                  reads=[n_b], writes=[xfb[s]])
                    for tb in range(4):
                        ys_ = k % 2
                        for g4 in range(4):
                            bank = 4 + (g4 % 4)
                            for q in range(4):
                                dc = g4 * 4 + q
                                P.pe(lambda e, s=s, tb=tb, dc=dc, q=q, bank=bank: e.transpose(
                                    ps[bank][:, q * 128:(q + 1) * 128], xf[s][:, dc, tb * 128:(tb + 1) * 128], ident_f[:]),
                                    reads=[xfb[s], cbuf], writes=[psb[bank]])
                            if g4 % 2 == 0:
                                P.dve(lambda e, ys_=ys_, g4=g4, bank=bank: e.tensor_copy(out=yt[ys_][:, g4 * 512:(g4 + 1) * 512], in_=ps[bank][:]),
                                      reads=[psb[bank]], writes=[ytb[ys_]])
                            else:
                                P.act(lambda e, ys_=ys_, g4=g4, bank=bank: e.activation(out=yt[ys_][:, g4 * 512:(g4 + 1) * 512], in_=ps[bank][:], func=AF.Copy),
                                      reads=[psb[bank]], writes=[ytb[ys_]])
                        t0 = tok0 + tb * 128
                        dstap = ys[t0:t0 + 128, :] if r == 0 else yp[t0 - 4096:t0 - 4096 + 128, :]
                        P.dma("sp", r_o, dstap, yt[ys_][:], reads=[ytb[ys_]])
                        k += 1
                P.barrier()

        def pass_ffn(m, l, s, src, dst, prev, conv_marks, out_ln=None):
            make_tables(l, s, prev, 0.5)
            with ExitStack() as st:
                h_t = [sb(f"h{i}", [128, DC, T], BF16, st=st) for i in range(2)]
                hb = [Buf(f"h{i}") for i in range(2)]
                g_t = sb("g", [128, FC, T], BF16, st=st)
                gb = Buf("g")
                NW = 4
                wsl = [sb(f"wsl{i}", [128, FC * 128], BF16, st=st) for i in range(NW)]
                wslb = [Buf(f"wsl{i}") for i in range(NW)]
                sil = [sb(f"sil{i}", [128, T], st=st) for i in range(2)]
                silb = [Buf(f"sil{i}") for i in range(2)]
                has_mod = mod_pos[0] < len(mod_items)
                mod_target = 72 if mod_pos[0] < 72 else len(mod_items)
                mod_limit[0] = mod_target
                if out_ln is not None:
                    zt = sb("zt_o", [128, DC, 128], st=st); ztb = Buf("zt_o")
                    yt_o = sb("yt_o", [128, D], st=st); ytb_o = Buf("yt_o")
                    ost = sb("ost", [128, 2, 128], st=st); ostb = Buf("ost")
                    oG = lngT[:, (out_ln[0] * 3 + out_ln[1]) * 16:(out_ln[0] * 3 + out_ln[1]) * 16 + 16]
                    oB = lnbT[:, (out_ln[0] * 3 + out_ln[1]) * 16:(out_ln[0] * 3 + out_ln[1]) * 16 + 16]

                    def out_block(ti, tb):
                        tok0, r = tiles[ti]
                        t0 = tok0 + tb * 128
                        for k in range(2):
                            P.dma("sp", r_st, ost[:, k, :], sts[dst][k, t0:t0 + 128].partition_broadcast(128),
                                  reads=[stbuf[dst][ti][k]], writes=[ostb])
                        P.dma("sp", r_x, zt[:], zs[dst][:, t0:t0 + 128].rearrange("(c p) t -> p c t", p=128),
                              reads=zbuf[dst][ti], writes=[ztb])
                        bc_t = lambda a: a.unsqueeze(1).broadcast_to([128, DC, 128])
                        bc_c = lambda a: a.unsqueeze(2).broadcast_to([128, DC, 128])
                        P.pool(lambda e: e.tensor_tensor(out=zt[:], in0=zt[:], in1=bc_t(ost[:, 0, :]), op=ALU.mult), reads=[ztb, ostb], writes=[ztb])
                        P.dve(lambda e: e.tensor_tensor(out=zt[:], in0=zt[:], in1=bc_t(ost[:, 1, :]), op=ALU.add), reads=[ztb, ostb], writes=[ztb])
                        P.pool(lambda e: e.tensor_tensor(out=zt[:], in0=zt[:], in1=bc_c(oG), op=ALU.mult), reads=[ztb, cbuf], writes=[ztb])
                        P.dve(lambda e: e.tensor_tensor(out=zt[:], in0=zt[:], in1=bc_c(oB), op=ALU.add), reads=[ztb, cbuf], writes=[ztb])

                    def out_block_b(ti, tb):
                        tok0, r = tiles[ti]
                        t0 = tok0 + tb * 128
                        for g4 in range(4):
                            bank = 6 + g4 % 2
                            for q in range(4):
                                dc = g4 * 4 + q
                                P.pe(lambda e, dc=dc, q=q, bank=bank: e.transpose(ps[bank][:, q * 128:(q + 1) * 128], zt[:, dc, :], ident_f[:]),
                                     reads=[ztb, cbuf], writes=[psb[bank]])
                            if g4 % 2 == 0:
                                P.dve(lambda e, g4=g4, bank=bank: e.tensor_copy(out=yt_o[:, g4 * 512:(g4 + 1) * 512], in_=ps[bank][:]),
                                      reads=[psb[bank]], writes=[ytb_o])
                            else:
                                P.act(lambda e, g4=g4, bank=bank: e.activation(out=yt_o[:, g4 * 512:(g4 + 1) * 512], in_=ps[bank][:], func=AF.Copy),
                                      reads=[psb[bank]], writes=[ytb_o])
                        dstap = ys[t0:t0 + 128, :] if r == 0 else yp[t0 - 4096:t0 - 4096 + 128, :]
                        P.dma("sp", r_o, dstap, yt_o[:], reads=[ytb_o])
                if has_mod:
                    wmf = [sb(f"wmf{i}", [128, 16, 256], BF16, st=st) for i in range(2)]
                    wmfb = [Buf(f"wmf{i}") for i in range(2)]
                wk = [0]
                def wload(kind, idx):
                    sl = wk[0] % NW
                    wk[0] += 1
                    if kind == 0:
                        P.dma("sp", r_w, wsl[sl][:, 0:4096], w13s[m, idx].rearrange("p a k n -> p (a k n)"),
                              reads=w13b[m][idx], writes=[wslb[sl]])
                    else:
                        P.dma("sp", r_w, wsl[sl][:, 0:FC * 128], w2s[m, idx].rearrange("p f n -> p (f n)"),
                              reads=w2b[m][idx], writes=[wslb[sl]])
                    return sl
                seq = [(0, f) for f in range(FC)] + [(1, d) for d in range(DC)]
                AHEAD = 3
                pending = []
                allseq = [(ti, kind, idx) for ti in range(NT) for (kind, idx) in seq]
                pos = [0]

                def prefetch_to(n):
                    while pos[0] < min(n, len(allseq)):
                        ti_, kind, idx = allseq[pos[0]]
                        pending.append(wload(kind, idx))
                        pos[0] += 1

                stsl = {}
                def do_prologue(ti, dcs):
                    if ti not in stsl:
                        stsl[ti] = load_stats(src, ti) if prev is not None else None
                    for dc in dcs:
                        prologue_dc(src, ti, stsl[ti], h_t[ti % 2], hb[ti % 2], dc)

                do_prologue(0, range(DC))
                done = 0
                total_conv = conv_marks[1] - conv_marks[0]
                conv_per_tile = -(-total_conv // (NT - 1)) if total_conv > 0 else 0
                for ti in range(NT):
                    hh, hhb = h_t[ti % 2], hb[ti % 2]
                    for f in range(FC):
                        prefetch_to(done + AHEAD)
                        sl = pending.pop(0)
                        done += 1
                        ab, bb = f % 2, 2 + f % 2
                        for kc in range(16):
                            P.pe(lambda e, sl=sl, kc=kc, ab=ab, hh=hh: e.matmul(ps[ab][:], wsl[sl][:, kc * 128:(kc + 1) * 128], hh[:, kc, :],
                                                                      start=(kc == 0), stop=(kc == 15)),
                                 reads=[wslb[sl], hhb], writes=[psb[ab]])
                        for kc in range(16):
                            P.pe(lambda e, sl=sl, kc=kc, bb=bb, hh=hh: e.matmul(ps[bb][:], wsl[sl][:, 2048 + kc * 128:2048 + (kc + 1) * 128], hh[:, kc, :],
                                                                      start=(kc == 0), stop=(kc == 15)),
                                 reads=[wslb[sl], hhb], writes=[psb[bb]])
                        si = f % 2
                        P.act(lambda e, si=si, ab=ab: e.activation(out=sil[si][:], in_=ps[ab][:], func=AF.Silu), reads=[psb[ab]], writes=[silb[si]])
                        P.dve(lambda e, si=si, bb=bb, f=f: e.tensor_tensor(out=g_t[:, f, :], in0=ps[bb][:], in1=sil[si][:], op=ALU.mult),
                              reads=[psb[bb], silb[si]], writes=[gb])
                        if 8 <= f < 8 + DC and ti + 1 < NT:
                            do_prologue(ti + 1, [f - 8])
                        if f % 4 == 1 and ti > 0:
                            conv_step(1 + conv_per_tile // 11)
                        if has_mod and f % 6 == 2 and mod_pos[0] < mod_target:
                            mod_step(1, wmf, wmfb, 7)
                        if out_ln is not None and ti > 0 and f >= 24 and (f - 24) % 5 == 0 and (f - 24) // 5 < 4:
                            out_block(ti - 1, (f - 24) // 5)
                        if out_ln is not None and ti > 0 and f >= 28 and (f - 28) % 5 == 0 and (f - 28) // 5 < 4:
                            out_block_b(ti - 1, (f - 28) // 5)
                    for dc in range(DC):
                        prefetch_to(done + AHEAD)
                        sl = pending.pop(0)
                        done += 1
                        yb = 4 + dc % 2
                        for f in range(FC):
                            P.pe(lambda e, sl=sl, f=f, yb=yb: e.matmul(ps[yb][:], wsl[sl][:, f * 128:(f + 1) * 128], g_t[:, f, :],
                                                                   start=(f == 0), stop=(f == FC - 1)),
                                 reads=[wslb[sl], gb], writes=[psb[yb]])
                        epilogue(src, dst, ti, dc, stsl[ti], prev, yb, dc == 0)
                    finalize(dst, ti)
                conv_until(conv_marks[1])
                if has_mod:
                    mod_step(mod_target - mod_pos[0], wmf, wmfb, 7, tail_prefetch=False)
                if out_ln is not None:
                    for tb in range(4):
                        out_block(NT - 1, tb)
                        out_block_b(NT - 1, tb)
                P.barrier()

        def pass_attn(l, src, dst, prev):
            make_tables(l, 1, prev, 1.0)
            norm_on_dve[0] = True
            with ExitStack() as st:
                KT = sb("KT", [128, NKV, NKC * 128], BF16, st=st); KTb = Buf("KT")
                V = sb("V", [128, NKC, 512], BF16, st=st); Vb = Buf("V")
                h2 = sb("h2", [128, DC, T], BF16, st=st); h2b = Buf("h2")
                QT = sb("QT", [128, NH, T], BF16, st=st); QTb = [Buf(f"QT{h}") for h in range(NH)]
                wq = [sb(f"wq{i}", [128, 16, 512], BF16, st=st) for i in range(2)]; wqb_ = [Buf(f"wqs{i}") for i in range(2)]
                wo = [sb(f"wo{i}", [128, 16, 128], BF16, st=st) for i in range(2)]; wob_ = [Buf(f"wos{i}") for i in range(2)]
                uA = [ntm[0], xtm_[0]]; uAb = [ntmb[0], xtmb[0]]
                uB = [ntm[1], xtm_[1]]; uBb = [ntmb[1], xtmb[1]]
                small = [sb(f"small{i}", [128, 12], st=st) for i in range(2)]; smallb = [Buf(f"small{i}") for i in range(2)]
                qbf = [sb(f"qbf{i}", [128, 512], BF16, st=st) for i in range(2)]; qbfb = [Buf(f"qbf{i}") for i in range(2)]
                pt = [sb(f"pt{i}", [128, 512], BF16, st=st) for i in range(4)]; ptb = [Buf(f"pt{i}") for i in range(4)]
                rtab = sb("rtab", [128, 4, 256], st=st); rtabb = Buf("rtab")
                KTp = KT[:, :, 0:T]; Vp = V[:, 0:4, :]
                psT = ps[7][:].bitcast(BF16)
                ctr = {"wq": 0, "wo": 0, "u": 0, "q": 0, "pt": 0, "s": 0}

                def load_wq(c):
                    s_ = ctr["wq"] % 2
                    ctr["wq"] += 1
                    P.dma("sp", r_w, wq[s_][:], wqs[c], reads=[wqb[c]], writes=[wqb_[s_]])
                    return s_

                def load_wo(dc):
                    s_ = ctr["wo"] % 2
                    ctr["wo"] += 1
                    P.dma("sp", r_w, wo[s_][:], wos[dc], reads=[wob[dc]], writes=[wob_[s_]])
                    return s_

                def proj_block(h_t, hb_, tb, ws, bank):
                    for kc in range(16):
                        P.pe(lambda e, kc=kc: e.matmul(ps[bank][:], h_t[:, kc, tb * 128:(tb + 1) * 128], wq[ws][:, kc, :],
                                                       start=(kc == 0), stop=(kc == 15)),
                             reads=[hb_, wqb_[ws]], writes=[psb[bank]])

                def unit(pbank, gain, rt, f32_out=None):
                    u = ctr["u"] % 2
                    ctr["u"] += 1
                    qi = ctr["q"] % 2
                    ctr["q"] += 1
                    tA, tAb, tB, tBb, sm, smb = uA[u], uAb[u], uB[u], uBb[u], small[u], smallb[u]
                    ob, obb = qbf[qi], qbfb[qi]
                    v4 = lambda t_: t_[:].rearrange("p (h d) -> p h d", h=4)
                    g4 = gain[:].unsqueeze(1).broadcast_to([128, 4, 128])
                    P.act(lambda e: e.activation(out=tA[:], in_=ps[pbank][:], func=AF.Square), reads=[psb[pbank]], writes=[tAb])
                    P.dve(lambda e: e.tensor_reduce(out=sm[:, 0:4], in_=v4(tA), axis=AX.X, op=ALU.add), reads=[tAb], writes=[smb])
                    P.dve(lambda e: e.tensor_scalar(out=sm[:, 4:8], in0=sm[:, 0:4], scalar1=1.0 / HD, scalar2=EPS_QK, op0=ALU.mult, op1=ALU.add),
                          reads=[smb], writes=[smb])
                    P.pool(lambda e: e.tensor_tensor(out=sm[:, 8:12], in0=sm[:, 4:8], in1=mhalf[:, 0:4], op=ALU.pow),
                           reads=[smb, cbuf], writes=[smb])
                    P.dve(lambda e: e.tensor_tensor(out=v4(tB), in0=ps[pbank][:].rearrange("p (h d) -> p h d", h=4),
                                                    in1=sm[:, 8:12].unsqueeze(2).broadcast_to([128, 4, 128]), op=ALU.mult),
                          reads=[psb[pbank], smb], writes=[tBb])
                    if rt is None:
                        if f32_out is not None:
                            fo, fob = f32_out
                            P.pool(lambda e: e.tensor_tensor(out=v4(fo), in0=v4(tB), in1=g4, op=ALU.mult), reads=[tBb, cbuf], writes=[fob])
                            P.pool(lambda e: e.tensor_copy(out=ob[:], in_=fo[:]), reads=[fob], writes=[obb])
                        else:
                            P.pool(lambda e: e.tensor_tensor(out=v4(ob), in0=v4(tB), in1=g4, op=ALU.mult), reads=[tBb, cbuf], writes=[obb])
                        return ob, obb
                    rt_ap, rt_b = rt
                    P.pool(lambda e: e.tensor_tensor(out=v4(tA), in0=v4(tB), in1=g4, op=ALU.mult), reads=[tBb, cbuf], writes=[tAb])
                    A5 = tA[:].rearrange("p (h f a i) -> p h f a i", h=4, f=2, a=2)
                    U5 = tB[:].rearrange("p (h f a i) -> p h f a i", h=4, f=2, a=2)
                    nsin = rt_ap[:, 128:192].rearrange("p (f i) -> p f i", f=2).unsqueeze(1).broadcast_to([128, 4, 2, 32])
                    psin = rt_ap[:, 192:256].rearrange("p (f i) -> p f i", f=2).unsqueeze(1).broadcast_to([128, 4, 2, 32])
                    cos4 = rt_ap[:, 0:128].unsqueeze(1).broadcast_to([128, 4, 128])
                    P.dve(lambda e: e.tensor_tensor(out=U5[:, :, :, 0, :], in0=A5[:, :, :, 1, :], in1=nsin, op=ALU.mult),
                          reads=[tAb, rt_b], writes=[tBb])
                    P.pool(lambda e: e.tensor_tensor(out=U5[:, :, :, 1, :], in0=A5[:, :, :, 0, :], in1=psin, op=ALU.mult),
                           reads=[tAb, rt_b], writes=[tBb])
                    P.pool(lambda e: e.tensor_tensor(out=v4(tA), in0=v4(tA), in1=cos4, op=ALU.mult), reads=[tAb, rt_b], writes=[tAb])
                    P.dve(lambda e: e.tensor_tensor(out=ob[:], in0=tA[:], in1=tB[:], op=ALU.add), reads=[tAb, tBb], writes=[obb])
                    return ob, obb

                def transpose4(src_bf, src_b, dst_ap, dst_bufs):
                    for j in range(4):
                        P.pe(lambda e, j=j: e.transpose(psT[:, j * 128:(j + 1) * 128], src_bf[:, j * 128:(j + 1) * 128], ident_b[:]),
                             reads=[src_b, cbuf], writes=[psb[7]])
                    P.act(lambda e: e.activation(out=dst_ap, in_=psT[:, 0:512].rearrange("p (h t) -> p h t", h=4), func=AF.Copy),
                          reads=[psb[7]], writes=dst_bufs)

                def cache_load():
                    for ch in range(2):
                        zi = nxt("zst", NZ)
                        P.dma("sp", r_zl, zst[zi][:], ck[ch * 128:(ch + 1) * 128, :], writes=[zstb[zi]])
                        qi = ctr["q"] % 2; ctr["q"] += 1
                        P.pool(lambda e, zi=zi, qi=qi: e.tensor_copy(out=qbf[qi][:], in_=zst[zi][:]), reads=[zstb[zi]], writes=[qbfb[qi]])
                        transpose4(qbf[qi], qbfb[qi], KT[:, :, ch * 128:(ch + 1) * 128], [KTb])
                        zi = nxt("zst", NZ)
                        P.dma("sp", r_zl, zst[zi][:], cv[ch * 128:(ch + 1) * 128, :], writes=[zstb[zi]])
                        P.pool(lambda e, zi=zi, ch=ch: e.tensor_copy(out=V[:, ch, :], in_=zst[zi][:]), reads=[zstb[zi]], writes=[Vb])

                def load_rtab(ti):
                    P.dma("sp", r_m, rtab[:], ropeR[ti * 4:(ti + 1) * 4].rearrange("g p f -> p g f"), writes=[rtabb])

                def kv_sample():
                    hbuf = [(h2, h2b), (QT, QTb[0])]
                    def pro(ti):
                        stslot = load_stats(src, ti)
                        prologue(src, ti, stslot, hbuf[ti % 2][0], hbuf[ti % 2][1])
                    pro(0)
                    for ti in range(NT_S):
                        hh, hhb = hbuf[ti % 2]
                        load_rtab(ti)
                        wk_ = load_wq(4)
                        wv_ = load_wq(5)
                        pend = None
                        for tb in range(4):
                            gbk = ti * 4 + tb
                            proj_block(hh, hhb, tb, wk_, 5)
                            proj_block(hh, hhb, tb, wv_, 6)
                            if pend is not None:
                                transpose4(*pend)
                            ob, obb = unit(5, kgain4, (rtab[:, tb, :], rtabb))
                            pend = (ob, obb, KT[:, :, (2 + gbk) * 128:(3 + gbk) * 128], [KTb])
                            P.act(lambda e, gbk=gbk: e.activation(out=V[:, 2 + gbk, :], in_=ps[6][:], func=AF.Copy), reads=[psb[6]], writes=[Vb])
                            if tb == 1 and ti + 1 < NT_S:
                                pro(ti + 1)
                        transpose4(*pend)
                        conv_step(2)

                def att_tile(ti):
                    tok0, r = tiles[ti]
                    stslot = load_stats(src, ti)
                    prologue(src, ti, stslot, h2, h2b)
                    if r == 0:
                        load_rtab(ti)
                    else:
                        wk_ = load_wq(4)
                        wv_ = load_wq(5)
                        pend = None
                        for tb in range(4):
                            proj_block(h2, h2b, tb, wk_, 5)
                            proj_block(h2, h2b, tb, wv_, 6)
                            if pend is not None:
                                transpose4(*pend)
                            kz = nxt("zn", 3)
                            ob, obb = unit(5, kgain4, None, f32_out=(zn[kz], znb[kz]))
                            t0 = tok0 - 4096 + tb * 128
                            P.dma("sp", r_o, okk[t0:t0 + 128, :], zn[kz][:], reads=[znb[kz]])
                            pend = (ob, obb, KTp[:, :, tb * 128:(tb + 1) * 128], [KTb])
                            vz = nxt("zn", 3)
                            P.act(lambda e, vz=vz: e.activation(out=zn[vz][:], in_=ps[6][:], func=AF.Copy), reads=[psb[6]], writes=[znb[vz]])
                            P.pool(lambda e, tb=tb, vz=vz: e.tensor_copy(out=Vp[:, tb, :], in_=zn[vz][:]), reads=[znb[vz]], writes=[Vb])
                            P.dma("sp", r_o, ovv[t0:t0 + 128, :], zn[vz][:], reads=[znb[vz]])
                        transpose4(*pend)

                    wslot = {}
                    wslot[0] = load_wq(0)

                    def q_piece(cb, tb):
                        bank = 5 + (cb * 4 + tb) % 2
                        proj_block(h2, h2b, tb, wslot[cb], bank)
                        rt = (rtab[:, tb, :], rtabb) if r == 0 else None
                        ob, obb = unit(bank, qgain4, rt)
                        return (ob, obb, QT[:, cb * 4:(cb + 1) * 4, tb * 128:(tb + 1) * 128], QTb[cb * 4:(cb + 1) * 4])

                    pend = None
                    for tb in range(4):
                        p_ = q_piece(0, tb)
                        if pend is not None:
                            transpose4(*pend)
                        pend = p_
                    transpose4(*pend)

                    def jobs_of(h):
                        g = h // 4
                        if r == 0:
                            return [(h, 0, T, [(KT[:, g, kc * 128:(kc + 1) * 128], V[:, kc, g * 128:(g + 1) * 128]) for kc in range(NKC)])]
                        return [(h, sq * 256, (sq + 1) * 256,
                                 [(KTp[:, g, tb * 128:(tb + 1) * 128], Vp[:, tb, g * 128:(g + 1) * 128]) for tb in (2 * sq, 2 * sq + 1)])
                                for sq in range(2)]

                    steps = []
                    for h in range(NH):
                        for (hh, q0, q1, chunks) in jobs_of(h):
                            for i, (kt, v) in enumerate(chunks):
                                steps.append((hh, q0, q1, i, len(chunks), kt, v))
                    sbase = ctr["s"]
                    ctr["s"] += len(steps)

                    def emit_S(k):
                        hh, q0, q1, i, n, kt, v = steps[k]
                        b_ = (sbase + k) % 3
                        P.pe(lambda e: e.matmul(ps[b_][:, 0:q1 - q0], kt, QT[:, hh, q0:q1], start=True, stop=True),
                             reads=[KTb, QTb[hh]], writes=[psb[b_]])

                    def head_epilogue(hh, q0, q1):
                        nq = q1 - q0
                        li = nxt("sq", 2)
                        oi = nxt("zn", 3)
                        P.dve(lambda e: e.tensor_copy(out=sqt[li][:, 0:nq], in_=ps[4][:, 0:nq]), reads=[psb[4]], writes=[sqb[li]])
                        P.dve(lambda e: e.tensor_copy(out=zn[oi][:, 0:nq], in_=ps[3][:, 0:nq]), reads=[psb[3]], writes=[znb[oi]])
                        P.dve(lambda e: e.reciprocal(out=sqt[li][:, 0:nq], in_=sqt[li][:, 0:nq]), reads=[sqb[li]], writes=[sqb[li]])
                        P.dve(lambda e: e.tensor_tensor(out=QT[:, hh, q0:q1], in0=zn[oi][:, 0:nq], in1=sqt[li][:, 0:nq], op=ALU.mult),
                              reads=[znb[oi], sqb[li]], writes=[QTb[hh]])

                    LA = 2
                    for k in range(min(LA, len(steps))):
                        emit_S(k)
                    pend = None
                    wo_slots = []
                    for k in range(len(steps)):
                        hh, q0, q1, i, n, kt, v = steps[k]
                        nq = q1 - q0
                        first_of_head = (i == 0 and q0 == 0)
                        last_of_head = (i == n - 1 and q1 == T)
                        cb = hh // 4
                        if first_of_head:
                            if hh % 4 == 0 and cb + 1 < 4:
                                wslot[cb + 1] = load_wq(cb + 1)
                            if pend is not None:
                                transpose4(*pend)
                                pend = None
                            if cb + 1 < 4:
                                pend = q_piece(cb + 1, hh % 4)
                            if hh == NH - 2:
                                wo_slots.append(load_wo(0))
                        if k + LA < len(steps):
                            nh_, nq0_, _, ni_, _, _, _ = steps[k + LA]
                            if ni_ == 0 and nq0_ == 0 and nh_ % 4 == 0 and pend is not None:
                                transpose4(*pend)
                                pend = None
                            emit_S(k + LA)
                        b_ = (sbase + k) % 3
                        pi = ctr["pt"] % 4
                        ctr["pt"] += 1
                        P.act(lambda e, pi=pi, b_=b_, nq=nq: e.activation(out=pt[pi][:, 0:nq], in_=ps[b_][:, 0:nq], func=AF.Exp, scale=SM_SCALE),
                              reads=[psb[b_]], writes=[ptb[pi]])
                        P.pe(lambda e, v=v, pi=pi, i=i, n=n, nq=nq: e.matmul(ps[3][:, 0:nq], v, pt[pi][:, 0:nq], start=(i == 0), stop=(i == n - 1)),
                             reads=[Vb, ptb[pi]], writes=[psb[3]])
                        P.pe(lambda e, pi=pi, i=i, n=n, nq=nq: e.matmul(ps[4][:, 0:nq], ones_b[:], pt[pi][:, 0:nq], start=(i == 0), stop=(i == n - 1)),
                             reads=[cbuf, ptb[pi]], writes=[psb[4]])
                        if i == n - 1:
                            head_epilogue(hh, q0, q1)
                        if last_of_head and hh % 4 == 3:
                            if pend is not None:
                                transpose4(*pend)
                                pend = None
                            conv_step(1)
                    for dc in range(DC):
                        s_ = wo_slots.pop(0)
                        if dc + 1 < DC:
                            wo_slots.append(load_wo(dc + 1))
                        yb = 5 + dc % 2
                        for hc in range(16):
                            P.pe(lambda e, s_=s_, hc=hc, yb=yb: e.matmul(ps[yb][:], wo[s_][:, hc, :], QT[:, hc, :], start=(hc == 0), stop=(hc == 15)),
                                 reads=[wob_[s_], QTb[hc]], writes=[psb[yb]])
                        epilogue(src, dst, ti, dc, stslot, prev, yb, dc == 0)
                    finalize(dst, ti, 3, 4)

                for ti in range(NT_S, NT):
                    att_tile(ti)
                cache_load()
                kv_sample()
                for ti in range(NT_S):
                    att_tile(ti)
                norm_on_dve[0] = False
                P.barrier()

        def pass_pool(l, src, dst, prev):
            make_tables(l, 1, prev, 1.0, use_pool_scale=True)
            with ExitStack() as st:
                PADW = 8
                W = T + 4 * PADW
                wp = sb("wp", [128, 16, 512], BF16, st=st); wpb = Buf("wp")
                P.dma("pool", r_cv, wp[:], w_pool.rearrange("g (kc p) n -> p (g kc) n", p=128), writes=[wpb])
                hst = sb("hst", [128, 2, 2, PADW], st=st); hstb = Buf("hst")
                hz = sb("hz", [128, 2, DC, PADW], st=st); hzb = [Buf("hz0"), Buf("hz1")]
                hp = sb("hp", [128, DC, W], st=st); hpb = [Buf(f"hp{g}") for g in range(4)]
                acc = [[sb(f"acc{g}_{i}", [128, 4, W], st=st) for i in range(2)] for g in range(2)]
                accb = [[Buf(f"acc{g}_{i}") for i in range(2)] for g in range(2)]
                d_t = [sb(f"d_t{i}", [128, DC, T], BF16, st=st) for i in range(2)]; d_b = [Buf(f"d_t{i}") for i in range(2)]
                fix = sb("fix", [128, 4, 2, 8], st=st); fixb = Buf("fix")
                for wi, w in enumerate(POOL_WINDOWS):
                    hw = w // 2
                    for t in range(8):
                        lf = (w / (t + hw)) if t < hw else 1.0
                        rf = (w / (hw + t + 1)) if t < hw - 1 else 1.0
                        P.pool(lambda e, wi=wi, t=t, lf=lf: e.memset(fix[:, wi, 0, t:t + 1], lf), writes=[fixb])
                        P.pool(lambda e, wi=wi, t=t, rf=rf: e.memset(fix[:, wi, 1, 7 - t:8 - t], rf), writes=[fixb])

                def fill(ti):
                    tok0, r = tiles[ti]
                    stslot = load_stats(src, ti)
                    has_l = (r == 0 and ti > 0)
                    has_r = (r == 0 and ti < NT_S - 1)
                    if r == 0:
                        segs = [(0, T)]
                    else:
                        segs = [(0, 256), (256, 256)]
                    L = segs[0][1]
                    SW = L + 2 * PADW
                    allhp = hpb
                    zero_cols = []
                    if r == 0:
                        if not has_l:
                            zero_cols.append((0, PADW))
                        if not has_r:
                            zero_cols.append((PADW + T, PADW + T + PADW))
                    else:
                        zero_cols = [(0, PADW), (PADW + 256, PADW + 256 + 2 * PADW), (2 * SW - PADW, 2 * SW)]
                    for (za, zb_) in zero_cols:
                        P.pool(lambda e, za=za, zb_=zb_: e.memset(hp[:, :, za:zb_], 0.0), writes=allhp)
                    for side, on, tcol, ocol in ((0, has_l, tok0 - PADW, 0), (1, has_r, tok0 + T, PADW + T)):
                        if not on:
                            continue
                        tn = ti - 1 if side == 0 else ti + 1
                        for k in range(2):
                            P.dma("sp", r_st, hst[:, side, k, :], sts[src][k, tcol:tcol + PADW].partition_broadcast(128),
                                  reads=[stbuf[src][tn][k]], writes=[hstb])
                        P.dma("sp", r_m, hz[:, side, :, :], zs[src][:, tcol:tcol + PADW].rearrange("(c p) t -> p c t", p=128),
                              reads=zbuf[src][tn], writes=[hzb[side]], allow_slow_non_contiguous=False)
                        hv = hz[:, side, :, :]
                        P.dve(lambda e, hv=hv, side=side: e.tensor_tensor(out=hv, in0=hv, in1=hst[:, side, 0, :].unsqueeze(1).broadcast_to([128, DC, PADW]), op=ALU.mult),
                              reads=[hzb[side], hstb], writes=[hzb[side]])
                        P.dve(lambda e, hv=hv, side=side: e.tensor_tensor(out=hv, in0=hv, in1=hst[:, side, 1, :].unsqueeze(1).broadcast_to([128, DC, PADW]), op=ALU.add),
                              reads=[hzb[side], hstb], writes=[hzb[side]])
                        P.dve(lambda e, hv=hv, r=r: e.tensor_tensor(out=hv, in0=hv, in1=tab[:, r, 0, :].unsqueeze(2).broadcast_to([128, DC, PADW]), op=ALU.mult),
                              reads=[hzb[side], tabb], writes=[hzb[side]])
                        P.dve(lambda e, hv=hv, r=r, ocol=ocol: e.tensor_tensor(out=hp[:, :, ocol:ocol + PADW], in0=hv,
                                                                            in1=tab[:, r, 1, :].unsqueeze(2).broadcast_to([128, DC, PADW]), op=ALU.add),
                              reads=[hzb[side], tabb], writes=allhp)
                    for dc in range(DC):
                        n_ap, n_b = load_norm(src, ti, dc, stslot)
                        for si_, (c0, Ls) in enumerate(segs):
                            o0 = si_ * SW + PADW
                            P.act(lambda e, o0=o0, c0=c0, Ls=Ls, n_ap=n_ap, dc=dc, r=r: e.activation(
                                out=hp[:, dc, o0:o0 + Ls], in_=n_ap[:, c0:c0 + Ls], func=AF.Identity,
                                scale=tab[:, r, 0, dc:dc + 1], bias=tab[:, r, 1, dc:dc + 1]),
                                reads=[n_b, tabb], writes=[hpb[dc // 4]])
                    return stslot

                def windows(ti, dsl):
                    tok0, r = tiles[ti]
                    has_l = (r == 0 and ti > 0)
                    has_r = (r == 0 and ti < NT_S - 1)
                    segs = [(0, T)] if r == 0 else [(0, 256), (256, 256)]
                    L = segs[0][1]
                    SW = L + 2 * PADW
                    Wt = len(segs) * SW
                    for gi in (3, 1, 2, 0):
                        w = POOL_WINDOWS[gi]
                        es = 0 if gi in (3, 1) else 1
                        eng = P.dve if es == 0 else P.pool
                        hg = hp[:, gi * 4:(gi + 1) * 4, :]
                        cur, curb = acc[es][0], accb[es][0]
                        oth, othb = acc[es][1], accb[es][1]
                        eng(lambda e, cur=cur, hg=hg, Wt=Wt: e.tensor_tensor(out=cur[:, :, 1:Wt], in0=hg[:, :, 0:Wt - 1], in1=hg[:, :, 1:Wt], op=ALU.add),
                            reads=[hpb[gi]], writes=[curb])
                        lo, hi_, step, ww = 1, Wt, 1, 2
                        while ww < w:
                            nlo, nhi = lo + step, hi_ - step
                            eng(lambda e, oth=oth, cur=cur, nlo=nlo, nhi=nhi, step=step: e.tensor_tensor(
                                out=oth[:, :, nlo:nhi], in0=cur[:, :, nlo - step:nhi - step], in1=cur[:, :, nlo + step:nhi + step], op=ALU.add),
                                reads=[curb], writes=[othb])
                            cur, curb, oth, othb = oth, othb, cur, curb
                            lo, hi_ = nlo, nhi
                            step *= 2
                            ww *= 2
                        for si_, (c0, Ls) in enumerate(segs):
                            o0 = si_ * SW + PADW
                            if not has_l:
                                eng(lambda e, cur=cur, o0=o0, gi=gi: e.tensor_tensor(out=cur[:, :, o0:o0 + 8], in0=cur[:, :, o0:o0 + 8],
                                                                                in1=fix[:, gi, 0, :].unsqueeze(1).broadcast_to([128, 4, 8]), op=ALU.mult),
                                    reads=[curb, fixb], writes=[curb])
                            if not has_r:
                                eng(lambda e, cur=cur, o0=o0, Ls=Ls, gi=gi: e.tensor_tensor(out=cur[:, :, o0 + Ls - 8:o0 + Ls], in0=cur[:, :, o0 + Ls - 8:o0 + Ls],
                                                                                      in1=fix[:, gi, 1, :].unsqueeze(1).broadcast_to([128, 4, 8]), op=ALU.mult),
                                    reads=[curb, fixb], writes=[curb])
                            P.dve(lambda e, cur=cur, hg=hg, o0=o0, Ls=Ls, c0=c0, gi=gi, w=w, dsl=dsl: e.scalar_tensor_tensor(
                                out=d_t[dsl][:, gi * 4:(gi + 1) * 4, c0:c0 + Ls], in0=cur[:, :, o0:o0 + Ls], scalar=1.0 / w, in1=hg[:, :, o0:o0 + Ls],
                                op0=ALU.mult, op1=ALU.subtract),
                                reads=[curb, hpb[gi]], writes=[d_b[dsl]])

                def mix(ti, dsl, stslot):
                    for dc in range(DC):
                        gi, q = dc // 4, dc % 4
                        yb = dc % 2
                        for kc in range(4):
                            P.pe(lambda e, gi=gi, q=q, kc=kc, yb=yb, dsl=dsl: e.matmul(ps[yb][:], wp[:, gi * 4 + kc, q * 128:(q + 1) * 128], d_t[dsl][:, gi * 4 + kc, :],
                                                                          start=(kc == 0), stop=(kc == 3)),
                                 reads=[wpb, d_b[dsl]], writes=[psb[yb]])
                        epilogue(src, dst, ti, dc, stslot, prev, yb, dc == 0, pe_stats=(6, 7, dc == DC - 1))
                    finalize(dst, ti, have_tot=True)

                sl = {}
                sl[0] = fill(0)
                windows(0, 0)
                for ti in range(NT):
                    if ti + 1 < NT:
                        sl[ti + 1] = fill(ti + 1)
                        windows(ti + 1, (ti + 1) % 2)
                    mix(ti, ti % 2, sl[ti])
                    conv_step(8)
                P.barrier()

        pass_in(0)
        if STAGE >= 1:
            pass_ffn(0, 0, 0, 0, 1, None, (mark_ffn0a, mark_att))
        if STAGE >= 2:
            conv_until(mark_att)
            pass_attn(0, 1, 0, (0, 0))
        if STAGE >= 3:
            conv_until(mark_ffn0b)
            pass_ffn(1, 0, 2, 0, 1, (0, 1), (mark_ffn0b, mark_ffn1a))
        if STAGE >= 4:
            conv_until(mark_ffn1a)
            pass_ffn(2, 1, 0, 1, 0, (0, 2), (mark_ffn1a, mark_ffn1b))
        if STAGE >= 5:
            pass_pool(1, 0, 1, (1, 0))
        if STAGE >= 6:
            conv_until(mark_ffn1b)
            pass_ffn(3, 1, 2, 1, 0, (1, 1), (mark_ffn1b, mark_ffn1b), out_ln=(1, 2))
        final_src, final_prev = {0: (0, None), 1: (1, (0, 0)), 2: (0, (0, 1)), 3: (1, (0, 2)), 4: (0, (1, 0)),
                                 5: (1, (1, 1))}.get(STAGE, (0, (1, 2)))
        if STAGE < 6:
            pass_out(final_src, final_prev)

        P.assign_tickets()
        with nc.Block() as block:
            @block.sync
            def _(e):
                P.emit("sp", e, esem)

            @block.scalar
            def _(e):
                P.emit("act", e, esem)

            @block.vector
            def _(e):
                P.emit("dve", e, esem)

            @block.gpsimd
            def _(e):
                P.emit("pool", e, esem)

            @block.tensor
            def _(e):
                P.emit("pe", e, esem)
    return nc


def _rope_tables():
    half = 32
    freqs = (np.float32(10000.0) ** (-np.arange(half, dtype=np.float32) / np.float32(half))).astype(np.float32)
    p = np.arange(128)
    col = (p % 64).astype(np.float32)
    angc = col[:, None] * freqs[None, :]
    tabR = np.zeros((32, 128, 8, 32), np.float32)
    for gb in range(32):
        row = (2 * gb + p // 64).astype(np.float32)
        angr = row[:, None] * freqs[None, :]
        tabR[gb, :, 0] = np.cos(angr); tabR[gb, :, 1] = np.cos(angr)
        tabR[gb, :, 2] = np.cos(angc); tabR[gb, :, 3] = np.cos(angc)
        tabR[gb, :, 4] = -np.sin(angr); tabR[gb, :, 5] = -np.sin(angc)
        tabR[gb, :, 6] = np.sin(angr); tabR[gb, :, 7] = np.sin(angc)
    return tabR.reshape(32, 128, 256)


_NC_CACHE = {}


def kernel(x_prompt, x_sample, cache_k, cache_v, c, c_ctx, w_mod, b_mod, ln_g, ln_b,
           ffn_w1, ffn_w3, ffn_w2, w_qkv, q_gain, k_gain, w_o, w_pool, pool_scale):
    f = lambda a: np.ascontiguousarray(np.asarray(a, dtype=np.float32))
    x_prompt, x_sample, cache_k, cache_v, c, c_ctx = map(f, (x_prompt, x_sample, cache_k, cache_v, c, c_ctx))
    tabR = _rope_tables()
    shared = {
        "w_mod": f(w_mod), "b_mod": f(b_mod).reshape(288, 128), "ln_g": f(ln_g).reshape(96, 128), "ln_b": f(ln_b).reshape(96, 128),
        "ffn_w1": f(ffn_w1).reshape(4, D, DFF), "ffn_w3": f(ffn_w3).reshape(4, D, DFF), "ffn_w2": f(ffn_w2).reshape(4, DFF, D),
        "w_qkv": f(w_qkv).reshape(D, 3072), "q_gain": f(q_gain).reshape(1, 128), "k_gain": f(k_gain).reshape(1, 128),
        "w_o": f(w_o).reshape(D, D), "w_pool": f(w_pool).reshape(4, 512, 512), "pool_scale": f(pool_scale).reshape(16, 128),
        "ropeR": tabR,
    }
    in_maps = []
    for j in range(NCORES):
        m = dict(shared)
        m["xs"] = x_sample[j]
        m["xp"] = x_prompt[4 * j:4 * j + 4].reshape(1024, D)
        m["ck"] = cache_k[j, 0].reshape(PAST, 512)
        m["cv"] = cache_v[j, 0].reshape(PAST, 512)
        m["cc"] = np.stack([c[j], c_ctx], 0).reshape(32, 128)
        in_maps.append(m)
    if "nc" not in _NC_CACHE:
        _NC_CACHE["nc"] = build_nc()
    nc = _NC_CACHE["nc"]
    res = run_bass_kernel_spmd(nc, in_maps, core_ids=list(range(NCORES)))
    y_prompt = np.empty((32, 256, D), np.float32)
    y_sample = np.empty((8, 4096, D), np.float32)
    ctx_k = np.empty((32, 1, 256, NKV, HD), np.float32)
    ctx_v = np.empty((32, 1, 256, NKV, HD), np.float32)
    y_prompt[:] = 0; y_sample[:] = 0; ctx_k[:] = 0; ctx_v[:] = 0
    for j in range(NCORES):
        r = res.results[j]
        y_sample[j] = r["ys"]
        y_prompt[4 * j:4 * j + 4] = r["yp"].reshape(4, 256, D)
        ctx_k[4 * j:4 * j + 4, 0] = r["ok"].reshape(4, 256, NKV, HD)
        ctx_v[4 * j:4 * j + 4, 0] = r["ov"].reshape(4, 256, NKV, HD)
    return (y_prompt, y_sample, ctx_k, ctx_v)
```

```python
import os
import numpy as np
from contextlib import ExitStack
import concourse.bass as bass
import concourse.mybir as mybir
from concourse.bass_utils import run_bass_kernel_spmd

F32 = mybir.dt.float32
BF16 = mybir.dt.bfloat16
AF = mybir.ActivationFunctionType
ALU = mybir.AluOpType
AX = mybir.AxisListType

D = 2048
DC = 16
DFF = 5632
FC = 44
T = 512
NT_S = int(os.environ.get("MK_NTS", "8"))
NT_P = int(os.environ.get("MK_NTP", "2"))
NCORES = int(os.environ.get("MK_CORES", "8"))
NT = NT_S + NT_P
NTOK = NT * T
HD = 128
NH = 16
NKV = 4
PAST = 256
NKC = (PAST + NT_S * T) // 128
ALPHA = 4.0 ** 0.25
EPS_LN = 1e-5 / (ALPHA * ALPHA)
EPS_QK = 1e-6
SM_SCALE = HD ** -0.5
POOL_WINDOWS = (2, 4, 8, 16)

STAGE = int(os.environ.get("MK_STAGE", "99"))


class Buf:
    __slots__ = ("name", "w", "r_eng", "r_dma")

    def __init__(self, name):
        self.name = name
        self.w = None
        self.r_eng = {}
        self.r_dma = []


class Op:
    __slots__ = ("eng", "fn", "is_dma", "deps", "signal", "ticket", "sem", "val", "idx")

    def __init__(self, eng, fn, is_dma):
        self.eng = eng
        self.fn = fn
        self.is_dma = is_dma
        self.deps = []
        self.signal = False
        self.ticket = 0
        self.sem = None
        self.val = 0


class Ring:
    def __init__(self, name, sems):
        self.name = name
        self.sems = sems
        self.ops = []


class Prog:
    ENGS = ("pe", "act", "dve", "pool", "sp")

    def __init__(self):
        self.ops = []
        self.rings = []
        self.last = {}

    def ring(self, name, sems):
        r = Ring(name, sems)
        self.rings.append(r)
        return r

    def add(self, eng, fn, reads=(), writes=(), ring=None, extra=()):
        is_dma = ring is not None
        op = Op(eng, fn, is_dma)
        op.idx = len(self.ops)
        deps = {}

        def need(d, raw):
            if d is None:
                return
            if (not d.is_dma) and (not is_dma) and d.eng == eng:
                if eng == "pe" or not raw:
                    return
            deps[id(d)] = d

        for b in reads:
            need(b.w, True)
        for b in writes:
            need(b.w, False)
            for d in b.r_eng.values():
                need(d, False)
            for d in b.r_dma:
                need(d, False)
        for d in extra:
            deps[id(d)] = d
        if is_dma:
            k = len(ring.ops)
            R = len(ring.sems)
            if k >= R:
                d = ring.ops[k - R]
                deps[id(d)] = d
            op.sem = ring.sems[k % R]
            op.val = 16 * (k // R + 1)
            ring.ops.append(op)
        for d in deps.values():
            d.signal = True
        op.deps = list(deps.values())
        for b in reads:
            if is_dma:
                b.r_dma.append(op)
            else:
                b.r_eng[eng] = op
        for b in writes:
            b.w = op
            b.r_eng = {}
            b.r_dma = []
        self.ops.append(op)
        if not is_dma:
            self.last[eng] = op
        return op

    def pe(self, fn, reads=(), writes=()):
        return self.add("pe", fn, reads, writes)

    def act(self, fn, reads=(), writes=()):
        return self.add("act", fn, reads, writes)

    def dve(self, fn, reads=(), writes=()):
        return self.add("dve", fn, reads, writes)

    def pool(self, fn, reads=(), writes=()):
        return self.add("pool", fn, reads, writes)

    def dma(self, queue, ring, out, in_, reads=(), writes=(), **kw):
        return self.add(queue, lambda e: e.dma_start(out=out, in_=in_, **kw), reads, writes, ring=ring)

    def barrier(self, skip=()):
        ds = [op for op in self.last.values()]
        for r in self.rings:
            if r in skip:
                continue
            ds.extend(r.ops[-len(r.sems):])
        for eng in self.ENGS:
            self.add(eng, None, extra=[d for d in ds if not (d.eng == eng and not d.is_dma)])
        self.last = {}

    def emit(self, eng, e, sems):
        known = {}
        for op in self.ops:
            if op.eng != eng:
                continue
            for d in op.deps:
                if d.is_dma:
                    s, v = d.sem, d.val
                else:
                    s, v = sems[d.eng], d.ticket
                key = id(s)
                if known.get(key, 0) >= v:
                    continue
                known[key] = v
                e.wait_ge(s, v)
            if op.fn is None:
                continue
            inst = op.fn(e)
            if op.is_dma:
                inst.then_inc(op.sem, 16)
            elif op.signal:
                inst.then_inc(sems[eng], 1)

    def assign_tickets(self):
        cnt = {}
        for op in self.ops:
            if op.is_dma or op.fn is None:
                continue
            if op.signal:
                cnt[op.eng] = cnt.get(op.eng, 0) + 1
                op.ticket = cnt[op.eng]
        return cnt


def build_nc():
    nc = bass.Bass("TRN2", target_bir_lowering=False)
    P = Prog()

    def din(name, shape, dt=F32):
        return nc.dram_tensor(name, list(shape), dt, kind="ExternalInput").ap()

    def dout(name, shape, dt=F32):
        return nc.dram_tensor(name, list(shape), dt, kind="ExternalOutput").ap()

    def dscr(name, shape, dt=F32):
        return nc.dram_tensor(name, list(shape), dt, kind="Internal").ap()

    xs = din("xs", [4096, D])
    xp = din("xp", [1024, D])
    ck = din("ck", [PAST, 512])
    cv = din("cv", [PAST, 512])
    cc = din("cc", [32, 128])
    w_mod = din("w_mod", [2, D, 9 * D])
    b_mod = din("b_mod", [288, 128])
    ln_g = din("ln_g", [96, 128])
    ln_b = din("ln_b", [96, 128])
    w1 = din("ffn_w1", [4, D, DFF])
    w3 = din("ffn_w3", [4, D, DFF])
    w2 = din("ffn_w2", [4, DFF, D])
    w_qkv = din("w_qkv", [D, 3072])
    q_gain = din("q_gain", [1, 128])
    k_gain = din("k_gain", [1, 128])
    w_o = din("w_o", [D, D])
    w_pool = din("w_pool", [4, 512, 512])
    pool_scale = din("pool_scale", [16, 128])
    ropeR = din("ropeR", [32, 128, 256])

    ys = dout("ys", [4096, D])
    yp = dout("yp", [1024, D])
    okk = dout("ok", [1024, 512])
    ovv = dout("ov", [1024, 512])

    zs = [dscr("zA", [D, 5120]), dscr("zB", [D, 5120])]
    sts = [dscr("stA", [2, 5120]), dscr("stB", [2, 5120])]
    w13s = dscr("w13s", [4, FC, 128, 2, 16, 128], BF16)
    w2s = dscr("w2s", [4, DC, 128, FC, 128], BF16)
    wqs = dscr("wqs", [6, 128, 16, 512], BF16)
    wos = dscr("wos", [DC, 128, 16, 128], BF16)

    zbuf = [[[Buf(f"z{a}_{t}_{d}") for d in range(DC)] for t in range(NT)] for a in range(2)]
    stbuf = [[[Buf(f"st{a}_{t}_{k}") for k in range(2)] for t in range(NT)] for a in range(2)]
    w13b = [[[Buf(f"w13_{m}_{f}_{k}") for k in range(2)] for f in range(FC)] for m in range(4)]
    w2b = [[[Buf(f"w2_{m}_{d}_{k}") for k in range(2)] for d in range(DC)] for m in range(4)]
    wqb = [Buf(f"wq_{c}") for c in range(6)]
    wob = [Buf(f"wo_{d}") for d in range(DC)]

    tiles = [(512 * i, 0) for i in range(NT_S)] + [(4096 + 512 * i, 1) for i in range(NT_P)]
    assert NTOK == NT * T

    with ExitStack() as gs:
        uniq = [0]

        def sb(name, shape, dt=F32, st=gs):
            uniq[0] += 1
            return st.enter_context(nc.sbuf_tensor(f"{name}_u{uniq[0]}", list(shape), dt))

        def sem(name):
            return gs.enter_context(nc.semaphore(name))

        esem = {e: sem("s_" + e) for e in ("pe", "act", "dve", "pool")}

        def mkring(name, n):
            return P.ring(name, [sem(f"r_{name}{i}") for i in range(n)])

        r_w = mkring("w", 4)
        r_zl = mkring("zl", 4)
        r_zs = mkring("zs", 4)
        r_cv = mkring("cv", 8)
        r_st = mkring("st", 4)
        r_m = mkring("m", 4)
        r_x = mkring("x", 2)
        r_o = mkring("o", 4)

        ps = [gs.enter_context(nc.psum_tensor(f"ps{i}", [128, 512], F32)) for i in range(8)]
        psb = [Buf(f"ps{i}") for i in range(8)]

        ident_f = sb("ident_f", [128, 128]); ident_b = sb("ident_b", [128, 128], BF16)
        ones_f = sb("ones_f", [128, 128]); ones_b = sb("ones_b", [128, 128], BF16)
        epsl = sb("epsl", [128, 1]); epsq = sb("epsq", [128, 1]); mhalf = sb("mhalf", [128, 4])
        cbuf = Buf("consts")
        itmp = sb("itmp", [128, 128])
        P.pool(lambda e: e.memset(ident_f[:], 0.0), writes=[cbuf])
        P.pool(lambda e: e.iota(itmp[:], pattern=[[1, 128]], base=0, channel_multiplier=-1,
                                allow_small_or_imprecise_dtypes=True), writes=[cbuf])
        P.pool(lambda e: e.tensor_scalar(out=ident_f[:], in0=itmp[:], scalar1=0.0, scalar2=None, op0=ALU.is_equal),
               reads=[cbuf], writes=[cbuf])
        P.pool(lambda e: e.tensor_copy(out=ident_b[:], in_=ident_f[:]), reads=[cbuf], writes=[cbuf])
        P.pool(lambda e: e.memset(ones_f[:], 1.0), writes=[cbuf])
        P.pool(lambda e: e.memset(ones_b[:], 1.0), writes=[cbuf])
        P.pool(lambda e: e.memset(epsl[:], EPS_LN), writes=[cbuf])
        P.pool(lambda e: e.memset(epsq[:], EPS_QK), writes=[cbuf])
        P.pool(lambda e: e.memset(mhalf[:], -0.5), writes=[cbuf])

        modT = sb("modT", [128, 2, 2, 9, 16])
        lngT = sb("lngT", [128, 96]); lnbT = sb("lnbT", [128, 96])
        pscT = sb("pscT", [128, 16])
        qgain4 = sb("qgain4", [128, 128]); kgain4 = sb("kgain4", [128, 128])
        modb = [[Buf(f"modT{l}{s_}") for s_ in range(3)] for l in range(2)]
        cndb = Buf("cond")

        def conv_w13_pair(m, f):
            for wi, wsrc in enumerate((w1, w3)):
                P.dma("pool", r_cv, w13s[m, f, :, wi, :, :],
                      wsrc[m, :, f * 128:(f + 1) * 128].rearrange("(kc p) n -> p kc n", p=128),
                      writes=[w13b[m][f][wi]])

        def conv_w2(m, d):
            for half in range(2):
                f0, f1 = half * 22, half * 22 + 22
                P.dma("pool", r_cv, w2s[m, d, :, f0:f1, :],
                      w2[m, f0 * 128:f1 * 128, d * 128:(d + 1) * 128].rearrange("(fc p) n -> p fc n", p=128),
                      writes=[w2b[m][d][half]])

        def conv_wq(c):
            P.dma("pool", r_cv, wqs[c], w_qkv[:, c * 512:(c + 1) * 512].rearrange("(kc p) n -> p kc n", p=128),
                  writes=[wqb[c]])

        def conv_wo(d):
            P.dma("pool", r_cv, wos[d], w_o[:, d * 128:(d + 1) * 128].rearrange("(kc p) n -> p kc n", p=128),
                  writes=[wob[d]])

        conv_list = []
        def queue_ffn_conv(m):
            for f in range(FC):
                conv_list.append(lambda m=m, f=f: conv_w13_pair(m, f))
            for d in range(DC):
                conv_list.append(lambda m=m, d=d: conv_w2(m, d))
        conv_pos = [0]

        def conv_step(n):
            for _ in range(n):
                if conv_pos[0] < len(conv_list):
                    conv_list[conv_pos[0]]()
                    conv_pos[0] += 1

        def conv_until(target):
            while conv_pos[0] < min(target, len(conv_list)):
                conv_list[conv_pos[0]]()
                conv_pos[0] += 1

        queue_ffn_conv(0)
        mark_ffn0a = len(conv_list)
        for c in (4, 5):
            conv_list.append(lambda c=c: conv_wq(c))
        for c in range(4):
            conv_list.append(lambda c=c: conv_wq(c))
        for d in range(DC):
            conv_list.append(lambda d=d: conv_wo(d))
        mark_att = len(conv_list)
        queue_ffn_conv(1)
        mark_ffn0b = len(conv_list)
        queue_ffn_conv(2)
        mark_ffn1a = len(conv_list)
        queue_ffn_conv(3)
        mark_ffn1b = len(conv_list)

        condT = sb("condT", [128, 32], BF16)
        bmT = sb("bmT", [128, 288])
        r_wm = mkring("wm", 2)
        cT = condT[:].rearrange("p (r k) -> p r k", r=2)
        mod_items = [(l, 3 * s_ + k, b_) for l in range(2) for s_ in range(3) for k in range(3) for b_ in range(8)]
        mod_pos = [0]

        mod_dma = [0]

        mod_limit = [24]

        def mod_issue(wm, wmb):
            k = mod_dma[0]
            if k >= min(len(mod_items), mod_limit[0]):
                return
            l, j, b_ = mod_items[k]
            sl_ = k % 2
            mod_dma[0] += 1
            c0 = j * D + b_ * 256
            P.dma("pool", r_wm, wm[sl_][:], w_mod[l, :, c0:c0 + 256].rearrange("(kc p) n -> p kc n", p=128), writes=[wmb[sl_]])

        def mod_step(n, wm, wmb, bank, tail_prefetch=True):
            for it_ in range(n):
                if mod_pos[0] >= len(mod_items):
                    return
                if mod_dma[0] <= mod_pos[0]:
                    mod_issue(wm, wmb)
                l, j, b_ = mod_items[mod_pos[0]]
                sl_ = mod_pos[0] % 2
                mod_pos[0] += 1
                for q in range(2):
                    for kc in range(16):
                        P.pe(lambda e, sl_=sl_, q=q, kc=kc: e.matmul(ps[bank][:, q * 2:q * 2 + 2], wm[sl_][:, kc, q * 128:(q + 1) * 128], cT[:, :, kc],
                                                                 start=(kc == 0), stop=(kc == 15)),
                             reads=[wmb[sl_], cndb], writes=[psb[bank]])
                o = (l * 9 + j) * 16 + 2 * b_
                for r in range(2):
                    P.dve(lambda e, l=l, j=j, b_=b_, r=r, o=o: e.tensor_tensor(
                        out=modT[:, r, l, j, 2 * b_:2 * b_ + 2],
                        in0=ps[bank][:, 0:4].rearrange("p (c r) -> p c r", r=2)[:, :, r],
                        in1=bmT[:, o:o + 2], op=ALU.add),
                        reads=[psb[bank], cndb], writes=[modb[l][j // 3]])
                if tail_prefetch or it_ < n - 1:
                    mod_issue(wm, wmb)

        ld = sb("ld_small", [128, 6, 128])
        ldb = Buf("ld_small")
        ld2 = sb("ld_small2", [128, 128])
        ld2b = Buf("ld2")
        sig = sb("sigc", [128, 128])
        if True:
            P.pool(lambda e: e.memset(ld[:], 0.0), writes=[ldb])
            P.dma("sp", r_m, ld[0:32, 0, :], cc[:, :], writes=[ldb])
            P.dma("sp", r_m, ld[0:96, 1, :], b_mod[0:96, :], writes=[ldb])
            P.dma("sp", r_m, ld[0:96, 2, :], b_mod[96:192, :], writes=[ldb])
            P.dma("sp", r_m, ld[0:96, 3, :], b_mod[192:288, :], writes=[ldb])
            P.dma("sp", r_m, ld[0:96, 4, :], ln_g[:, :], writes=[ldb])
            P.dma("sp", r_m, ld[0:96, 5, :], ln_b[:, :], writes=[ldb])
            P.pool(lambda e: e.memset(ld2[:], 0.0), writes=[ld2b])
            P.dma("sp", r_m, ld2[0:16, :], pool_scale[:, :], writes=[ld2b])
            P.dma("sp", r_m, qgain4[:], q_gain[0, :].partition_broadcast(128), writes=[cbuf])
            P.dma("sp", r_m, kgain4[:], k_gain[0, :].partition_broadcast(128), writes=[cbuf])
            P.act(lambda e: e.activation(out=sig[:], in_=ld[:, 0, :], func=AF.Silu), reads=[ldb], writes=[ldb])
            P.pe(lambda e: e.transpose(ps[0][:, 0:128], sig[:], ident_f[:]), reads=[ldb, cbuf], writes=[psb[0]])
            for i in range(1, 6):
                P.pe(lambda e, i=i: e.transpose(ps[i // 4][:, (i % 4) * 128:(i % 4 + 1) * 128], ld[:, i, :], ident_f[:]),
                     reads=[ldb, cbuf], writes=[psb[i // 4]])
            P.pe(lambda e: e.transpose(ps[1][:, 256:384], ld2[:], ident_f[:]), reads=[ld2b, cbuf], writes=[psb[1]])
            P.dve(lambda e: e.tensor_copy(out=condT[:], in_=ps[0][:, 0:32]), reads=[psb[0]], writes=[cndb])
            for i in range(3):
                P.dve(lambda e, i=i: e.tensor_copy(out=bmT[:, i * 96:(i + 1) * 96], in_=ps[0][:, (i + 1) * 128:(i + 1) * 128 + 96]),
                      reads=[psb[0]], writes=[cndb])
            P.dve(lambda e: e.tensor_copy(out=lngT[:], in_=ps[1][:, 0:96]), reads=[psb[1]], writes=[cbuf])
            P.dve(lambda e: e.tensor_copy(out=lnbT[:], in_=ps[1][:, 128:224]), reads=[psb[1]], writes=[cbuf])
            P.dve(lambda e: e.tensor_copy(out=pscT[:], in_=ps[1][:, 256:272]), reads=[psb[1]], writes=[cbuf])
        with ExitStack() as s0:
            wm0 = [sb(f"wm{i}", [128, 16, 256], BF16, st=s0) for i in range(2)]
            wm0b = [Buf(f"wm{i}") for i in range(2)]
            mod_step(24, wm0, wm0b, 2, tail_prefetch=False)
            conv_until(mark_ffn0a)
            P.barrier(skip=[r_cv])

        NZ = 4
        zst = [sb(f"zst{i}", [128, T]) for i in range(NZ)]; zstb = [Buf(f"zst{i}") for i in range(NZ)]
        ntm = [sb(f"ntm{i}", [128, T]) for i in range(2)]; ntmb = [Buf(f"ntm{i}") for i in range(2)]
        xtm_ = [sb(f"xe{i}", [128, T]) for i in range(2)]; xtmb = [Buf(f"xe{i}") for i in range(2)]
        zn = [sb(f"zn{i}", [128, T]) for i in range(3)]; znb = [Buf(f"zn{i}") for i in range(3)]
        sqt = [sb(f"sq{i}", [128, T]) for i in range(2)]; sqb = [Buf(f"sq{i}") for i in range(2)]
        S1 = sb("S1", [128, T]); S2 = sb("S2", [128, T]); S1b = Buf("S1"); S2b = Buf("S2")
        stat_in = [[sb(f"sti{i}_{k}", [128, T]) for k in range(2)] for i in range(2)]
        stat_inb = [[Buf(f"sti{i}_{k}") for k in range(2)] for i in range(2)]
        mean_t = sb("mean_t", [128, T]); rstd_t = sb("rstd_t", [128, T]); nmr_t = sb("nmr_t", [128, T])
        msq_t = rstd_t; var_t = nmr_t
        finb = Buf("fin")
        tab = sb("tab", [128, 2, 3, 16])
        tabb = Buf("tab")
        cnt = {"zst": 0, "ntm": 0, "xe": 0, "zn": 0, "sq": 0, "sti": 0}

        def nxt(k, n):
            v = cnt[k] % n
            cnt[k] += 1
            return v

        def make_tables(l, s, prev, wgt, use_pool_scale=False):
            for r in range(2):
                sh = modT[:, r, l, 3 * s + 0, :]
                sc = modT[:, r, l, 3 * s + 1, :]
                gt = modT[:, r, l, 3 * s + 2, :]
                if prev is None:
                    P.dve(lambda e, r=r, sc=sc: e.tensor_scalar(out=tab[:, r, 0, :], in0=sc, scalar1=1.0, scalar2=None, op0=ALU.add),
                          reads=[modb[l][s]], writes=[tabb])
                    P.dve(lambda e, r=r, sh=sh: e.tensor_copy(out=tab[:, r, 1, :], in_=sh), reads=[modb[l][s]], writes=[tabb])
                else:
                    o = (prev[0] * 3 + prev[1]) * 16
                    G = lngT[:, o:o + 16]
                    Bv = lnbT[:, o:o + 16]
                    P.dve(lambda e, r=r, sc=sc, G=G: e.scalar_tensor_tensor(out=tab[:, r, 0, :], in0=sc, scalar=1.0, in1=G,
                                                                        op0=ALU.add, op1=ALU.mult),
                          reads=[modb[l][s], cbuf], writes=[tabb])
                    P.dve(lambda e, r=r, sc=sc, Bv=Bv: e.scalar_tensor_tensor(out=tab[:, r, 1, :], in0=sc, scalar=1.0, in1=Bv,
                                                                          op0=ALU.add, op1=ALU.mult),
                          reads=[modb[l][s], cbuf], writes=[tabb])
                    P.dve(lambda e, r=r, sh=sh: e.tensor_tensor(out=tab[:, r, 1, :], in0=tab[:, r, 1, :], in1=sh, op=ALU.add),
                          reads=[modb[l][s], tabb], writes=[tabb])
                if use_pool_scale:
                    P.dve(lambda e, r=r, gt=gt: e.scalar_tensor_tensor(out=tab[:, r, 2, :], in0=gt, scalar=wgt / ALPHA, in1=pscT[:],
                                                                   op0=ALU.mult, op1=ALU.mult),
                          reads=[modb[l][s], cbuf], writes=[tabb])
                else:
                    P.dve(lambda e, r=r, gt=gt: e.tensor_scalar(out=tab[:, r, 2, :], in0=gt, scalar1=wgt / ALPHA, scalar2=None,
                                                            op0=ALU.mult),
                          reads=[modb[l][s]], writes=[tabb])

        def load_stats(src, ti):
            tok0 = tiles[ti][0]
            s = nxt("sti", 2)
            for k in range(2):
                P.dma("sp", r_st, stat_in[s][k][:], sts[src][k, tok0:tok0 + T].partition_broadcast(128),
                      reads=[stbuf[src][ti][k]], writes=[stat_inb[s][k]])
            return s

        norm_on_dve = [False]

        def load_norm(src, ti, dc, stslot, c0=0, c1=T, tcol=None):
            tok0 = tiles[ti][0]
            zi = nxt("zst", NZ)
            P.dma("sp", r_zl, zst[zi][:, c0:c1], zs[src][dc * 128:(dc + 1) * 128, tok0 + c0:tok0 + c1],
                  reads=[zbuf[src][ti][dc]], writes=[zstb[zi]])
            if stslot is None:
                return zst[zi], zstb[zi]
            ni = nxt("ntm", 2)
            (P.dve if norm_on_dve[0] else P.pool)(
                lambda e: e.tensor_tensor(out=zst[zi][:, c0:c1], in0=zst[zi][:, c0:c1], in1=stat_in[stslot][0][:, c0:c1], op=ALU.mult),
                reads=[zstb[zi], stat_inb[stslot][0]], writes=[zstb[zi]])
            P.dve(lambda e: e.tensor_tensor(out=ntm[ni][:, c0:c1], in0=zst[zi][:, c0:c1], in1=stat_in[stslot][1][:, c0:c1], op=ALU.add),
                  reads=[zstb[zi], stat_inb[stslot][1]], writes=[ntmb[ni]])
            return ntm[ni], ntmb[ni]

        def prologue_dc(src, ti, stslot, h_t, hb, dc):
            r = tiles[ti][1]
            n_ap, n_b = load_norm(src, ti, dc, stslot)
            P.act(lambda e, dc=dc, n_ap=n_ap: e.activation(out=h_t[:, dc, :], in_=n_ap[:], func=AF.Identity,
                                                         scale=tab[:, r, 0, dc:dc + 1], bias=tab[:, r, 1, dc:dc + 1]),
                  reads=[n_b, tabb], writes=[hb])

        def prologue(src, ti, stslot, h_t, hb):
            for dc in range(DC):
                prologue_dc(src, ti, stslot, h_t, hb, dc)

        def epi_begin():
            pass

        def epilogue(src, dst, ti, dc, stslot, prev, ybank, first, s_eng=None, pe_stats=None):
            tok0, r = tiles[ti]
            n_ap, n_b = load_norm(src, ti, dc, stslot)
            if prev is not None:
                xi = nxt("xe", 2)
                o = (prev[0] * 3 + prev[1]) * 16 + dc
                P.act(lambda e: e.activation(out=xtm_[xi][:], in_=n_ap[:], func=AF.Identity,
                                             scale=lngT[:, o:o + 1], bias=lnbT[:, o:o + 1]),
                      reads=[n_b, cbuf], writes=[xtmb[xi]])
                x_ap, x_b = xtm_[xi], xtmb[xi]
            else:
                x_ap, x_b = n_ap, n_b
            zi = nxt("zn", 3)
            P.dve(lambda e: e.scalar_tensor_tensor(out=zn[zi][:], in0=ps[ybank][:], scalar=tab[:, r, 2, dc:dc + 1], in1=x_ap[:],
                                                   op0=ALU.mult, op1=ALU.add),
                  reads=[psb[ybank], x_b, tabb], writes=[znb[zi]])
            si = nxt("sq", 2)
            P.act(lambda e: e.activation(out=sqt[si][:], in_=zn[zi][:], func=AF.Square), reads=[znb[zi]], writes=[sqb[si]])
            se = s_eng or P.pool
            if pe_stats is not None:
                b1, b2, last = pe_stats
                P.pe(lambda e: e.matmul(ps[b1][:], ones_f[:], zn[zi][:], start=first, stop=last), reads=[znb[zi], cbuf], writes=[psb[b1]])
                P.pe(lambda e: e.matmul(ps[b2][:], ones_f[:], sqt[si][:], start=first, stop=last), reads=[sqb[si], cbuf], writes=[psb[b2]])
            elif first:
                se(lambda e: e.tensor_copy(out=S1[:], in_=zn[zi][:]), reads=[znb[zi]], writes=[S1b])
                se(lambda e: e.tensor_copy(out=S2[:], in_=sqt[si][:]), reads=[sqb[si]], writes=[S2b])
            else:
                se(lambda e: e.tensor_tensor(out=S1[:], in0=S1[:], in1=zn[zi][:], op=ALU.add), reads=[znb[zi], S1b], writes=[S1b])
                se(lambda e: e.tensor_tensor(out=S2[:], in0=S2[:], in1=sqt[si][:], op=ALU.add), reads=[sqb[si], S2b], writes=[S2b])
            P.dma("act", r_zs, zs[dst][dc * 128:(dc + 1) * 128, tok0:tok0 + T], zn[zi][:],
                  reads=[znb[zi]], writes=[zbuf[dst][ti][dc]])

        def finalize(dst, ti, b1=6, b2=7, have_tot=False):
            tok0 = tiles[ti][0]
            if not have_tot:
                P.pe(lambda e: e.matmul(ps[b1][:], ones_f[:], S1[:], start=True, stop=True), reads=[S1b, cbuf], writes=[psb[b1]])
                P.pe(lambda e: e.matmul(ps[b2][:], ones_f[:], S2[:], start=True, stop=True), reads=[S2b, cbuf], writes=[psb[b2]])
            P.act(lambda e: e.activation(out=mean_t[:], in_=ps[b1][:], func=AF.Copy, scale=1.0 / D), reads=[psb[b1]], writes=[finb])
            P.dve(lambda e: e.tensor_tensor(out=msq_t[:], in0=mean_t[:], in1=mean_t[:], op=ALU.mult), reads=[finb], writes=[finb])
            P.dve(lambda e: e.scalar_tensor_tensor(out=var_t[:], in0=ps[b2][:], scalar=1.0 / D, in1=msq_t[:],
                                                   op0=ALU.mult, op1=ALU.subtract), reads=[psb[b2], finb], writes=[finb])
            P.act(lambda e: e.activation(out=var_t[:], in_=var_t[:], func=AF.Sqrt, bias=epsl[:, 0:1], scale=1.0),
                  reads=[finb, cbuf], writes=[finb])
            P.dve(lambda e: e.reciprocal(out=rstd_t[:], in_=var_t[:]), reads=[finb], writes=[finb])
            P.dve(lambda e: e.scalar_tensor_tensor(out=nmr_t[:], in0=mean_t[:], scalar=-1.0, in1=rstd_t[:],
                                                   op0=ALU.mult, op1=ALU.mult), reads=[finb], writes=[finb])
            P.dma("sp", r_st, sts[dst][0:1, tok0:tok0 + T], rstd_t[0:1, :], reads=[finb], writes=[stbuf[dst][ti][0]])
            P.dma("sp", r_st, sts[dst][1:2, tok0:tok0 + T], nmr_t[0:1, :], reads=[finb], writes=[stbuf[dst][ti][1]])

        def pass_in(dst):
            with ExitStack() as st:
                xin = [sb(f"xin{i}", [128, 4, D], st=st) for i in range(2)]
                xinb = [Buf(f"xin{i}") for i in range(2)]
                k = 0
                for ti in range(NT):
                    tok0, r = tiles[ti]
                    s = ti % 2
                    srcap = xs[tok0:tok0 + T, :] if r == 0 else xp[tok0 - 4096:tok0 - 4096 + T, :]
                    P.dma("sp", r_x, xin[s][:], srcap.rearrange("(tb p) d -> p tb d", p=128), writes=[xinb[s]])
                    for dc in range(DC):
                        bank = 4 + (k % 4)
                        for tb in range(4):
                            P.pe(lambda e, s=s, tb=tb, dc=dc, bank=bank: e.transpose(
                                ps[bank][:, tb * 128:(tb + 1) * 128], xin[s][:, tb, dc * 128:(dc + 1) * 128], ident_f[:]),
                                reads=[xinb[s], cbuf], writes=[psb[bank]])
                        zi = nxt("zn", 3)
                        if k % 2 == 0:
                            P.dve(lambda e, zi=zi, bank=bank: e.tensor_copy(out=zn[zi][:], in_=ps[bank][:]), reads=[psb[bank]], writes=[znb[zi]])
                        else:
                            P.act(lambda e, zi=zi, bank=bank: e.activation(out=zn[zi][:], in_=ps[bank][:], func=AF.Copy), reads=[psb[bank]], writes=[znb[zi]])
                        P.dma("act", r_zs, zs[dst][dc * 128:(dc + 1) * 128, tok0:tok0 + T], zn[zi][:],
                              reads=[znb[zi]], writes=[zbuf[dst][ti][dc]])
                        k += 1
                P.barrier(skip=[r_cv])

        def pass_out(src, prev):
            with ExitStack() as st:
                xf = [sb(f"xf{i}", [128, DC, T], st=st) for i in range(2)]
                xfb = [Buf(f"xf{i}") for i in range(2)]
                yt = [sb(f"yt{i}", [128, D], st=st) for i in range(2)]
                ytb = [Buf(f"yt{i}") for i in range(2)]
                k = 0
                for ti in range(NT):
                    tok0, r = tiles[ti]
                    s = ti % 2
                    stslot = load_stats(src, ti) if prev is not None else None
                    for dc in range(DC):
                        n_ap, n_b = load_norm(src, ti, dc, stslot)
                        if prev is not None:
                            o = (prev[0] * 3 + prev[1]) * 16 + dc
                            P.act(lambda e, dc=dc, n_ap=n_ap, o=o, s=s: e.activation(
                                out=xf[s][:, dc, :], in_=n_ap[:], func=AF.Identity, scale=lngT[:, o:o + 1], bias=lnbT[:, o:o + 1]),
                                reads=[n_b, cbuf], writes=[xfb[s]])
                        else:
                            P.act(lambda e, dc=dc, n_ap=n_ap, s=s: e.activation(out=xf[s][:, dc, :], in_=n_ap[:], func=AF.Copy),
                                  reads=[n_b], writes=[xfb[s]])
                    for tb in range(4):
                        ys_ = k % 2
                        for g4 in range(4):
                            bank = 4 + (g4 % 4)
                            for q in range(4):
                                dc = g4 * 4 + q
                                P.pe(lambda e, s=s, tb=tb, dc=dc, q=q, bank=bank: e.transpose(
                                    ps[bank][:, q * 128:(q + 1) * 128], xf[s][:, dc, tb * 128:(tb + 1) * 128], ident_f[:]),
                                    reads=[xfb[s], cbuf], writes=[psb[bank]])
                            if g4 % 2 == 0:
                                P.dve(lambda e, ys_=ys_, g4=g4, bank=bank: e.tensor_copy(out=yt[ys_][:, g4 * 512:(g4 + 1) * 512], in_=ps[bank][:]),
                                      reads=[psb[bank]], writes=[ytb[ys_]])
                            else:
                                P.act(lambda e, ys_=ys_, g4=g4, bank=bank: e.activation(out=yt[ys_][:, g4 * 512:(g4 + 1) * 512], in_=ps[bank][:], func=AF.Copy),
                                      reads=[psb[bank]], writes=[ytb[ys_]])
                        t0 = tok0 + tb * 128
                        dstap = ys[t0:t0 + 128, :] if r == 0 else yp[t0 - 4096:t0 - 4096 + 128, :]
                        P.dma("sp", r_o, dstap, yt[ys_][:], reads=[ytb[ys_]])
                        k += 1
                P.barrier()

        def pass_ffn(m, l, s, src, dst, prev, conv_marks, out_ln=None):
            make_tables(l, s, prev, 0.5)
            with ExitStack() as st:
                h_t = [sb(f"h{i}", [128, DC, T], BF16, st=st) for i in range(2)]
                hb = [Buf(f"h{i}") for i in range(2)]
                g_t = sb("g", [128, FC, T], BF16, st=st)
                gb = Buf("g")
                NW = 4
                wsl = [sb(f"wsl{i}", [128, FC * 128], BF16, st=st) for i in range(NW)]
                wslb = [Buf(f"wsl{i}") for i in range(NW)]
                sil = [sb(f"sil{i}", [128, T], st=st) for i in range(2)]
                silb = [Buf(f"sil{i}") for i in range(2)]
                has_mod = mod_pos[0] < len(mod_items)
                mod_target = 72 if mod_pos[0] < 72 else len(mod_items)
                mod_limit[0] = mod_target
                if out_ln is not None:
                    zt = sb("zt_o", [128, DC, 128], st=st); ztb = Buf("zt_o")
                    yt_o = sb("yt_o", [128, D], st=st); ytb_o = Buf("yt_o")
                    ost = sb("ost", [128, 2, 128], st=st); ostb = Buf("ost")
                    oG = lngT[:, (out_ln[0] * 3 + out_ln[1]) * 16:(out_ln[0] * 3 + out_ln[1]) * 16 + 16]
                    oB = lnbT[:, (out_ln[0] * 3 + out_ln[1]) * 16:(out_ln[0] * 3 + out_ln[1]) * 16 + 16]

                    def out_block(ti, tb):
                        tok0, r = tiles[ti]
                        t0 = tok0 + tb * 128
                        for k in range(2):
                            P.dma("sp", r_st, ost[:, k, :], sts[dst][k, t0:t0 + 128].partition_broadcast(128),
                                  reads=[stbuf[dst][ti][k]], writes=[ostb])
                        P.dma("sp", r_x, zt[:], zs[dst][:, t0:t0 + 128].rearrange("(c p) t -> p c t", p=128),
                              reads=zbuf[dst][ti], writes=[ztb])
                        bc_t = lambda a: a.unsqueeze(1).broadcast_to([128, DC, 128])
                        bc_c = lambda a: a.unsqueeze(2).broadcast_to([128, DC, 128])
                        P.pool(lambda e: e.tensor_tensor(out=zt[:], in0=zt[:], in1=bc_t(ost[:, 0, :]), op=ALU.mult), reads=[ztb, ostb], writes=[ztb])
                        P.dve(lambda e: e.tensor_tensor(out=zt[:], in0=zt[:], in1=bc_t(ost[:, 1, :]), op=ALU.add), reads=[ztb, ostb], writes=[ztb])
                        P.pool(lambda e: e.tensor_tensor(out=zt[:], in0=zt[:], in1=bc_c(oG), op=ALU.mult), reads=[ztb, cbuf], writes=[ztb])
                        P.dve(lambda e: e.tensor_tensor(out=zt[:], in0=zt[:], in1=bc_c(oB), op=ALU.add), reads=[ztb, cbuf], writes=[ztb])

                    def out_block_b(ti, tb):
                        tok0, r = tiles[ti]
                        t0 = tok0 + tb * 128
                        for g4 in range(4):
                            bank = 6 + g4 % 2
                            for q in range(4):
                                dc = g4 * 4 + q
                                P.pe(lambda e, dc=dc, q=q, bank=bank: e.transpose(ps[bank][:, q * 128:(q + 1) * 128], zt[:, dc, :], ident_f[:]),
                                     reads=[ztb, cbuf], writes=[psb[bank]])
                            if g4 % 2 == 0:
                                P.dve(lambda e, g4=g4, bank=bank: e.tensor_copy(out=yt_o[:, g4 * 512:(g4 + 1) * 512], in_=ps[bank][:]),
                                      reads=[psb[bank]], writes=[ytb_o])
                            else:
                                P.act(lambda e, g4=g4, bank=bank: e.activation(out=yt_o[:, g4 * 512:(g4 + 1) * 512], in_=ps[bank][:], func=AF.Copy),
                                      reads=[psb[bank]], writes=[ytb_o])
                        dstap = ys[t0:t0 + 128, :] if r == 0 else yp[t0 - 4096:t0 - 4096 + 128, :]
                        P.dma("sp", r_o, dstap, yt_o[:], reads=[ytb_o])
                if has_mod:
                    wmf = [sb(f"wmf{i}", [128, 16, 256], BF16, st=st) for i in range(2)]
                    wmfb = [Buf(f"wmf{i}") for i in range(2)]
                wk = [0]
                def wload(kind, idx):
                    sl = wk[0] % NW
                    wk[0] += 1
                    if kind == 0:
                        P.dma("sp", r_w, wsl[sl][:, 0:4096], w13s[m, idx].rearrange("p a k n -> p (a k n)"),
                              reads=w13b[m][idx], writes=[wslb[sl]])
                    else:
                        P.dma("sp", r_w, wsl[sl][:, 0:FC * 128], w2s[m, idx].rearrange("p f n -> p (f n)"),
                              reads=w2b[m][idx], writes=[wslb[sl]])
                    return sl
                seq = [(0, f) for f in range(FC)] + [(1, d) for d in range(DC)]
                AHEAD = 3
                pending = []
                allseq = [(ti, kind, idx) for ti in range(NT) for (kind, idx) in seq]
                pos = [0]

                def prefetch_to(n):
                    while pos[0] < min(n, len(allseq)):
                        ti_, kind, idx = allseq[pos[0]]
                        pending.append(wload(kind, idx))
                        pos[0] += 1

                stsl = {}
                def do_prologue(ti, dcs):
                    if ti not in stsl:
                        stsl[ti] = load_stats(src, ti) if prev is not None else None
                    for dc in dcs:
                        prologue_dc(src, ti, stsl[ti], h_t[ti % 2], hb[ti % 2], dc)

                do_prologue(0, range(DC))
                done = 0
                total_conv = conv_marks[1] - conv_marks[0]
                conv_per_tile = -(-total_conv // (NT - 1)) if total_conv > 0 else 0
                for ti in range(NT):
                    hh, hhb = h_t[ti % 2], hb[ti % 2]
                    for f in range(FC):
                        prefetch_to(done + AHEAD)
                        sl = pending.pop(0)
                        done += 1
                        ab, bb = f % 2, 2 + f % 2
                        for kc in range(16):
                            P.pe(lambda e, sl=sl, kc=kc, ab=ab, hh=hh: e.matmul(ps[ab][:], wsl[sl][:, kc * 128:(kc + 1) * 128], hh[:, kc, :],
                                                                      start=(kc == 0), stop=(kc == 15)),
                                 reads=[wslb[sl], hhb], writes=[psb[ab]])
                        for kc in range(16):
                            P.pe(lambda e, sl=sl, kc=kc, bb=bb, hh=hh: e.matmul(ps[bb][:], wsl[sl][:, 2048 + kc * 128:2048 + (kc + 1) * 128], hh[:, kc, :],
                                                                      start=(kc == 0), stop=(kc == 15)),
                                 reads=[wslb[sl], hhb], writes=[psb[bb]])
                        si = f % 2
                        P.act(lambda e, si=si, ab=ab: e.activation(out=sil[si][:], in_=ps[ab][:], func=AF.Silu), reads=[psb[ab]], writes=[silb[si]])
                        P.dve(lambda e, si=si, bb=bb, f=f: e.tensor_tensor(out=g_t[:, f, :], in0=ps[bb][:], in1=sil[si][:], op=ALU.mult),
                              reads=[psb[bb], silb[si]], writes=[gb])
                        if 8 <= f < 8 + DC and ti + 1 < NT:
                            do_prologue(ti + 1, [f - 8])
                        if f % 4 == 1 and ti > 0:
                            conv_step(1 + conv_per_tile // 11)
                        if has_mod and f % 6 == 2 and mod_pos[0] < mod_target:
                            mod_step(1, wmf, wmfb, 7)
                        if out_ln is not None and ti > 0 and f >= 24 and (f - 24) % 5 == 0 and (f - 24) // 5 < 4:
                            out_block(ti - 1, (f - 24) // 5)
                        if out_ln is not None and ti > 0 and f >= 28 and (f - 28) % 5 == 0 and (f - 28) // 5 < 4:
                            out_block_b(ti - 1, (f - 28) // 5)
                    for dc in range(DC):
                        prefetch_to(done + AHEAD)
                        sl = pending.pop(0)
                        done += 1
                        yb = 4 + dc % 2
                        for f in range(FC):
                            P.pe(lambda e, sl=sl, f=f, yb=yb: e.matmul(ps[yb][:], wsl[sl][:, f * 128:(f + 1) * 128], g_t[:, f, :],
                                                                   start=(f == 0), stop=(f == FC - 1)),
                                 reads=[wslb[sl], gb], writes=[psb[yb]])
                        epilogue(src, dst, ti, dc, stsl[ti], prev, yb, dc == 0)
                    finalize(dst, ti)
                conv_until(conv_marks[1])
                if has_mod:
                    mod_step(mod_target - mod_pos[0], wmf, wmfb, 7, tail_prefetch=False)
                if out_ln is not None:
                    for tb in range(4):
                        out_block(NT - 1, tb)
                        out_block_b(NT - 1, tb)
                P.barrier()

        def pass_attn(l, src, dst, prev):
            make_tables(l, 1, prev, 1.0)
            norm_on_dve[0] = True
            with ExitStack() as st:
                KT = sb("KT", [128, NKV, NKC * 128], BF16, st=st); KTb = Buf("KT")
                V = sb("V", [128, NKC, 512], BF16, st=st); Vb = Buf("V")
                h2 = sb("h2", [128, DC, T], BF16, st=st); h2b = Buf("h2")
                QT = sb("QT", [128, NH, T], BF16, st=st); QTb = [Buf(f"QT{h}") for h in range(NH)]
                wq = [sb(f"wq{i}", [128, 16, 512], BF16, st=st) for i in range(2)]; wqb_ = [Buf(f"wqs{i}") for i in range(2)]
                wo = [sb(f"wo{i}", [128, 16, 128], BF16, st=st) for i in range(2)]; wob_ = [Buf(f"wos{i}") for i in range(2)]
                uA = [ntm[0], xtm_[0]]; uAb = [ntmb[0], xtmb[0]]
                uB = [ntm[1], xtm_[1]]; uBb = [ntmb[1], xtmb[1]]
                small = [sb(f"small{i}", [128, 12], st=st) for i in range(2)]; smallb = [Buf(f"small{i}") for i in range(2)]
                qbf = [sb(f"qbf{i}", [128, 512], BF16, st=st) for i in range(2)]; qbfb = [Buf(f"qbf{i}") for i in range(2)]
                pt = [sb(f"pt{i}", [128, 512], BF16, st=st) for i in range(4)]; ptb = [Buf(f"pt{i}") for i in range(4)]
                rtab = sb("rtab", [128, 4, 256], st=st); rtabb = Buf("rtab")
                KTp = KT[:, :, 0:T]; Vp = V[:, 0:4, :]
                psT = ps[7][:].bitcast(BF16)
                ctr = {"wq": 0, "wo": 0, "u": 0, "q": 0, "pt": 0, "s": 0}

                def load_wq(c):
                    s_ = ctr["wq"] % 2
                    ctr["wq"] += 1
                    P.dma("sp", r_w, wq[s_][:], wqs[c], reads=[wqb[c]], writes=[wqb_[s_]])
                    return s_

                def load_wo(dc):
                    s_ = ctr["wo"] % 2
                    ctr["wo"] += 1
                    P.dma("sp", r_w, wo[s_][:], wos[dc], reads=[wob[dc]], writes=[wob_[s_]])
                    return s_

                def proj_block(h_t, hb_, tb, ws, bank):
                    for kc in range(16):
                        P.pe(lambda e, kc=kc: e.matmul(ps[bank][:], h_t[:, kc, tb * 128:(tb + 1) * 128], wq[ws][:, kc, :],
                                                       start=(kc == 0), stop=(kc == 15)),
                             reads=[hb_, wqb_[ws]], writes=[psb[bank]])

                def unit(pbank, gain, rt, f32_out=None):
                    u = ctr["u"] % 2
                    ctr["u"] += 1
                    qi = ctr["q"] % 2
                    ctr["q"] += 1
                    tA, tAb, tB, tBb, sm, smb = uA[u], uAb[u], uB[u], uBb[u], small[u], smallb[u]
                    ob, obb = qbf[qi], qbfb[qi]
                    v4 = lambda t_: t_[:].rearrange("p (h d) -> p h d", h=4)
                    g4 = gain[:].unsqueeze(1).broadcast_to([128, 4, 128])
                    P.act(lambda e: e.activation(out=tA[:], in_=ps[pbank][:], func=AF.Square), reads=[psb[pbank]], writes=[tAb])
                    P.dve(lambda e: e.tensor_reduce(out=sm[:, 0:4], in_=v4(tA), axis=AX.X, op=ALU.add), reads=[tAb], writes=[smb])
                    P.dve(lambda e: e.tensor_scalar(out=sm[:, 4:8], in0=sm[:, 0:4], scalar1=1.0 / HD, scalar2=EPS_QK, op0=ALU.mult, op1=ALU.add),
                          reads=[smb], writes=[smb])
                    P.pool(lambda e: e.tensor_tensor(out=sm[:, 8:12], in0=sm[:, 4:8], in1=mhalf[:, 0:4], op=ALU.pow),
                           reads=[smb, cbuf], writes=[smb])
                    P.dve(lambda e: e.tensor_tensor(out=v4(tB), in0=ps[pbank][:].rearrange("p (h d) -> p h d", h=4),
                                                    in1=sm[:, 8:12].unsqueeze(2).broadcast_to([128, 4, 128]), op=ALU.mult),
                          reads=[psb[pbank], smb], writes=[tBb])
                    if rt is None:
                        if f32_out is not None:
                            fo, fob = f32_out
                            P.pool(lambda e: e.tensor_tensor(out=v4(fo), in0=v4(tB), in1=g4, op=ALU.mult), reads=[tBb, cbuf], writes=[fob])
                            P.pool(lambda e: e.tensor_copy(out=ob[:], in_=fo[:]), reads=[fob], writes=[obb])
                        else:
                            P.pool(lambda e: e.tensor_tensor(out=v4(ob), in0=v4(tB), in1=g4, op=ALU.mult), reads=[tBb, cbuf], writes=[obb])
                        return ob, obb
                    rt_ap, rt_b = rt
                    P.pool(lambda e: e.tensor_tensor(out=v4(tA), in0=v4(tB), in1=g4, op=ALU.mult), reads=[tBb, cbuf], writes=[tAb])
                    A5 = tA[:].rearrange("p (h f a i) -> p h f a i", h=4, f=2, a=2)
                    U5 = tB[:].rearrange("p (h f a i) -> p h f a i", h=4, f=2, a=2)
                    nsin = rt_ap[:, 128:192].rearrange("p (f i) -> p f i", f=2).unsqueeze(1).broadcast_to([128, 4, 2, 32])
                    psin = rt_ap[:, 192:256].rearrange("p (f i) -> p f i", f=2).unsqueeze(1).broadcast_to([128, 4, 2, 32])
                    cos4 = rt_ap[:, 0:128].unsqueeze(1).broadcast_to([128, 4, 128])
                    P.dve(lambda e: e.tensor_tensor(out=U5[:, :, :, 0, :], in0=A5[:, :, :, 1, :], in1=nsin, op=ALU.mult),
                          reads=[tAb, rt_b], writes=[tBb])
                    P.pool(lambda e: e.tensor_tensor(out=U5[:, :, :, 1, :], in0=A5[:, :, :, 0, :], in1=psin, op=ALU.mult),
                           reads=[tAb, rt_b], writes=[tBb])
                    P.pool(lambda e: e.tensor_tensor(out=v4(tA), in0=v4(tA), in1=cos4, op=ALU.mult), reads=[tAb, rt_b], writes=[tAb])
                    P.dve(lambda e: e.tensor_tensor(out=ob[:], in0=tA[:], in1=tB[:], op=ALU.add), reads=[tAb, tBb], writes=[obb])
                    return ob, obb

                def transpose4(src_bf, src_b, dst_ap, dst_bufs):
                    for j in range(4):
                        P.pe(lambda e, j=j: e.transpose(psT[:, j * 128:(j + 1) * 128], src_bf[:, j * 128:(j + 1) * 128], ident_b[:]),
                             reads=[src_b, cbuf], writes=[psb[7]])
                    P.act(lambda e: e.activation(out=dst_ap, in_=psT[:, 0:512].rearrange("p (h t) -> p h t", h=4), func=AF.Copy),
                          reads=[psb[7]], writes=dst_bufs)

                def cache_load():
                    for ch in range(2):
                        zi = nxt("zst", NZ)
                        P.dma("sp", r_zl, zst[zi][:], ck[ch * 128:(ch + 1) * 128, :], writes=[zstb[zi]])
                        qi = ctr["q"] % 2; ctr["q"] += 1
                        P.pool(lambda e, zi=zi, qi=qi: e.tensor_copy(out=qbf[qi][:], in_=zst[zi][:]), reads=[zstb[zi]], writes=[qbfb[qi]])
                        transpose4(qbf[qi], qbfb[qi], KT[:, :, ch * 128:(ch + 1) * 128], [KTb])
                        zi = nxt("zst", NZ)
                        P.dma("sp", r_zl, zst[zi][:], cv[ch * 128:(ch + 1) * 128, :], writes=[zstb[zi]])
                        P.pool(lambda e, zi=zi, ch=ch: e.tensor_copy(out=V[:, ch, :], in_=zst[zi][:]), reads=[zstb[zi]], writes=[Vb])

                def load_rtab(ti):
                    P.dma("sp", r_m, rtab[:], ropeR[ti * 4:(ti + 1) * 4].rearrange("g p f -> p g f"), writes=[rtabb])

                def kv_sample():
                    hbuf = [(h2, h2b), (QT, QTb[0])]
                    def pro(ti):
                        stslot = load_stats(src, ti)
                        prologue(src, ti, stslot, hbuf[ti % 2][0], hbuf[ti % 2][1])
                    pro(0)
                    for ti in range(NT_S):
                        hh, hhb = hbuf[ti % 2]
                        load_rtab(ti)
                        wk_ = load_wq(4)
                        wv_ = load_wq(5)
                        pend = None
                        for tb in range(4):
                            gbk = ti * 4 + tb
                            proj_block(hh, hhb, tb, wk_, 5)
                            proj_block(hh, hhb, tb, wv_, 6)
                            if pend is not None:
                                transpose4(*pend)
                            ob, obb = unit(5, kgain4, (rtab[:, tb, :], rtabb))
                            pend = (ob, obb, KT[:, :, (2 + gbk) * 128:(3 + gbk) * 128], [KTb])
                            P.act(lambda e, gbk=gbk: e.activation(out=V[:, 2 + gbk, :], in_=ps[6][:], func=AF.Copy), reads=[psb[6]], writes=[Vb])
                            if tb == 1 and ti + 1 < NT_S:
                                pro(ti + 1)
                        transpose4(*pend)
                        conv_step(2)

                def att_tile(ti):
                    tok0, r = tiles[ti]
                    stslot = load_stats(src, ti)
                    prologue(src, ti, stslot, h2, h2b)
                    if r == 0:
                        load_rtab(ti)
                    else:
                        wk_ = load_wq(4)
                        wv_ = load_wq(5)
                        pend = None
                        for tb in range(4):
                            proj_block(h2, h2b, tb, wk_, 5)
                            proj_block(h2, h2b, tb, wv_, 6)
                            if pend is not None:
                                transpose4(*pend)
                            kz = nxt("zn", 3)
                            ob, obb = unit(5, kgain4, None, f32_out=(zn[kz], znb[kz]))
                            t0 = tok0 - 4096 + tb * 128
                            P.dma("sp", r_o, okk[t0:t0 + 128, :], zn[kz][:], reads=[znb[kz]])
                            pend = (ob, obb, KTp[:, :, tb * 128:(tb + 1) * 128], [KTb])
                            vz = nxt("zn", 3)
                            P.act(lambda e, vz=vz: e.activation(out=zn[vz][:], in_=ps[6][:], func=AF.Copy), reads=[psb[6]], writes=[znb[vz]])
                            P.pool(lambda e, tb=tb, vz=vz: e.tensor_copy(out=Vp[:, tb, :], in_=zn[vz][:]), reads=[znb[vz]], writes=[Vb])
                            P.dma("sp", r_o, ovv[t0:t0 + 128, :], zn[vz][:], reads=[znb[vz]])
                        transpose4(*pend)

                    wslot = {}
                    wslot[0] = load_wq(0)

                    def q_piece(cb, tb):
                        bank = 5 + (cb * 4 + tb) % 2
                        proj_block(h2, h2b, tb, wslot[cb], bank)
                        rt = (rtab[:, tb, :], rtabb) if r == 0 else None
                        ob, obb = unit(bank, qgain4, rt)
                        return (ob, obb, QT[:, cb * 4:(cb + 1) * 4, tb * 128:(tb + 1) * 128], QTb[cb * 4:(cb + 1) * 4])

                    pend = None
                    for tb in range(4):
                        p_ = q_piece(0, tb)
                        if pend is not None:
                            transpose4(*pend)
                        pend = p_
                    transpose4(*pend)

                    def jobs_of(h):
                        g = h // 4
                        if r == 0:
                            return [(h, 0, T, [(KT[:, g, kc * 128:(kc + 1) * 128], V[:, kc, g * 128:(g + 1) * 128]) for kc in range(NKC)])]
                        return [(h, sq * 256, (sq + 1) * 256,
                                 [(KTp[:, g, tb * 128:(tb + 1) * 128], Vp[:, tb, g * 128:(g + 1) * 128]) for tb in (2 * sq, 2 * sq + 1)])
                                for sq in range(2)]

                    steps = []
                    for h in range(NH):
                        for (hh, q0, q1, chunks) in jobs_of(h):
                            for i, (kt, v) in enumerate(chunks):
                                steps.append((hh, q0, q1, i, len(chunks), kt, v))
                    sbase = ctr["s"]
                    ctr["s"] += len(steps)

                    def emit_S(k):
                        hh, q0, q1, i, n, kt, v = steps[k]
                        b_ = (sbase + k) % 3
                        P.pe(lambda e: e.matmul(ps[b_][:, 0:q1 - q0], kt, QT[:, hh, q0:q1], start=True, stop=True),
                             reads=[KTb, QTb[hh]], writes=[psb[b_]])

                    def head_epilogue(hh, q0, q1):
                        nq = q1 - q0
                        li = nxt("sq", 2)
                        oi = nxt("zn", 3)
                        P.dve(lambda e: e.tensor_copy(out=sqt[li][:, 0:nq], in_=ps[4][:, 0:nq]), reads=[psb[4]], writes=[sqb[li]])
                        P.dve(lambda e: e.tensor_copy(out=zn[oi][:, 0:nq], in_=ps[3][:, 0:nq]), reads=[psb[3]], writes=[znb[oi]])
                        P.dve(lambda e: e.reciprocal(out=sqt[li][:, 0:nq], in_=sqt[li][:, 0:nq]), reads=[sqb[li]], writes=[sqb[li]])
                        P.dve(lambda e: e.tensor_tensor(out=QT[:, hh, q0:q1], in0=zn[oi][:, 0:nq], in1=sqt[li][:, 0:nq], op=ALU.mult),
                              reads=[znb[oi], sqb[li]], writes=[QTb[hh]])

                    LA = 2
                    for k in range(min(LA, len(steps))):
                        emit_S(k)
                    pend = None
                    wo_slots = []
                    for k in range(len(steps)):
                        hh, q0, q1, i, n, kt, v = steps[k]
                        nq = q1 - q0
                        first_of_head = (i == 0 and q0 == 0)
                        last_of_head = (i == n - 1 and q1 == T)
                        cb = hh // 4
                        if first_of_head:
                            if hh % 4 == 0 and cb + 1 < 4:
                                wslot[cb + 1] = load_wq(cb + 1)
                            if pend is not None:
                                transpose4(*pend)
                                pend = None
                            if cb + 1 < 4:
                                pend = q_piece(cb + 1, hh % 4)
                            if hh == NH - 2:
                                wo_slots.append(load_wo(0))
                        if k + LA < len(steps):
                            nh_, nq0_, _, ni_, _, _, _ = steps[k + LA]
                            if ni_ == 0 and nq0_ == 0 and nh_ % 4 == 0 and pend is not None:
                                transpose4(*pend)
                                pend = None
                            emit_S(k + LA)
                        b_ = (sbase + k) % 3
                        pi = ctr["pt"] % 4
                        ctr["pt"] += 1
                        P.act(lambda e, pi=pi, b_=b_, nq=nq: e.activation(out=pt[pi][:, 0:nq], in_=ps[b_][:, 0:nq], func=AF.Exp, scale=SM_SCALE),
                              reads=[psb[b_]], writes=[ptb[pi]])
                        P.pe(lambda e, v=v, pi=pi, i=i, n=n, nq=nq: e.matmul(ps[3][:, 0:nq], v, pt[pi][:, 0:nq], start=(i == 0), stop=(i == n - 1)),
                             reads=[Vb, ptb[pi]], writes=[psb[3]])
                        P.pe(lambda e, pi=pi, i=i, n=n, nq=nq: e.matmul(ps[4][:, 0:nq], ones_b[:], pt[pi][:, 0:nq], start=(i == 0), stop=(i == n - 1)),
                             reads=[cbuf, ptb[pi]], writes=[psb[4]])
                        if i == n - 1:
                            head_epilogue(hh, q0, q1)
                        if last_of_head and hh % 4 == 3:
                            if pend is not None:
                                transpose4(*pend)
                                pend = None
                            conv_step(1)
                    for dc in range(DC):
                        s_ = wo_slots.pop(0)
                        if dc + 1 < DC:
                            wo_slots.append(load_wo(dc + 1))
                        yb = 5 + dc % 2
                        for hc in range(16):
                            P.pe(lambda e, s_=s_, hc=hc, yb=yb: e.matmul(ps[yb][:], wo[s_][:, hc, :], QT[:, hc, :], start=(hc == 0), stop=(hc == 15)),
                                 reads=[wob_[s_], QTb[hc]], writes=[psb[yb]])
                        epilogue(src, dst, ti, dc, stslot, prev, yb, dc == 0)
                    finalize(dst, ti, 3, 4)

                for ti in range(NT_S, NT):
                    att_tile(ti)
                cache_load()
                kv_sample()
                for ti in range(NT_S):
                    att_tile(ti)
                norm_on_dve[0] = False
                P.barrier()

        def pass_pool(l, src, dst, prev):
            make_tables(l, 1, prev, 1.0, use_pool_scale=True)
            with ExitStack() as st:
                PADW = 8
                W = T + 4 * PADW
                wp = sb("wp", [128, 16, 512], BF16, st=st); wpb = Buf("wp")
                P.dma("pool", r_cv, wp[:], w_pool.rearrange("g (kc p) n -> p (g kc) n", p=128), writes=[wpb])
                hst = sb("hst", [128, 2, 2, PADW], st=st); hstb = Buf("hst")
                hz = sb("hz", [128, 2, DC, PADW], st=st); hzb = [Buf("hz0"), Buf("hz1")]
                hp = sb("hp", [128, DC, W], st=st); hpb = [Buf(f"hp{g}") for g in range(4)]
                acc = [[sb(f"acc{g}_{i}", [128, 4, W], st=st) for i in range(2)] for g in range(2)]
                accb = [[Buf(f"acc{g}_{i}") for i in range(2)] for g in range(2)]
                d_t = [sb(f"d_t{i}", [128, DC, T], BF16, st=st) for i in range(2)]; d_b = [Buf(f"d_t{i}") for i in range(2)]
                fix = sb("fix", [128, 4, 2, 8], st=st); fixb = Buf("fix")
                for wi, w in enumerate(POOL_WINDOWS):
                    hw = w // 2
                    for t in range(8):
                        lf = (w / (t + hw)) if t < hw else 1.0
                        rf = (w / (hw + t + 1)) if t < hw - 1 else 1.0
                        P.pool(lambda e, wi=wi, t=t, lf=lf: e.memset(fix[:, wi, 0, t:t + 1], lf), writes=[fixb])
                        P.pool(lambda e, wi=wi, t=t, rf=rf: e.memset(fix[:, wi, 1, 7 - t:8 - t], rf), writes=[fixb])

                def fill_pre(ti):
                    tok0, r = tiles[ti]
                    stslot = load_stats(src, ti)
                    has_l = (r == 0 and ti > 0)
                    has_r = (r == 0 and ti < NT_S - 1)
                    if r == 0:
                        segs = [(0, T)]
                    else:
                        segs = [(0, 256), (256, 256)]
                    L = segs[0][1]
                    SW = L + 2 * PADW
                    allhp = hpb
                    zero_cols = []
                    if r == 0:
                        if not has_l:
                            zero_cols.append((0, PADW))
                        if not has_r:
                            zero_cols.append((PADW + T, PADW + T + PADW))
                    else:
                        zero_cols = [(0, PADW), (PADW + 256, PADW + 256 + 2 * PADW), (2 * SW - PADW, 2 * SW)]
                    for (za, zb_) in zero_cols:
                        P.pool(lambda e, za=za, zb_=zb_: e.memset(hp[:, :, za:zb_], 0.0), writes=allhp)
                    for side, on, tcol, ocol in ((0, has_l, tok0 - PADW, 0), (1, has_r, tok0 + T, PADW + T)):
                        if not on:
                            continue
                        tn = ti - 1 if side == 0 else ti + 1
                        for k in range(2):
                            P.dma("sp", r_st, hst[:, side, k, :], sts[src][k, tcol:tcol + PADW].partition_broadcast(128),
                                  reads=[stbuf[src][tn][k]], writes=[hstb])
                        P.dma("sp", r_m, hz[:, side, :, :], zs[src][:, tcol:tcol + PADW].rearrange("(c p) t -> p c t", p=128),
                              reads=zbuf[src][tn], writes=[hzb[side]], allow_slow_non_contiguous=False)
                        hv = hz[:, side, :, :]
                        P.dve(lambda e, hv=hv, side=side: e.tensor_tensor(out=hv, in0=hv, in1=hst[:, side, 0, :].unsqueeze(1).broadcast_to([128, DC, PADW]), op=ALU.mult),
                              reads=[hzb[side], hstb], writes=[hzb[side]])
                        P.dve(lambda e, hv=hv, side=side: e.tensor_tensor(out=hv, in0=hv, in1=hst[:, side, 1, :].unsqueeze(1).broadcast_to([128, DC, PADW]), op=ALU.add),
                              reads=[hzb[side], hstb], writes=[hzb[side]])
                        P.dve(lambda e, hv=hv, r=r: e.tensor_tensor(out=hv, in0=hv, in1=tab[:, r, 0, :].unsqueeze(2).broadcast_to([128, DC, PADW]), op=ALU.mult),
                              reads=[hzb[side], tabb], writes=[hzb[side]])
                        P.dve(lambda e, hv=hv, r=r, ocol=ocol: e.tensor_tensor(out=hp[:, :, ocol:ocol + PADW], in0=hv,
                                                                            in1=tab[:, r, 1, :].unsqueeze(2).broadcast_to([128, DC, PADW]), op=ALU.add),
                              reads=[hzb[side], tabb], writes=allhp)
                    return (ti, r, stslot, segs, SW)

                def fill_dc(ctx, dc):
                    ti, r, stslot, segs, SW = ctx
                    n_ap, n_b = load_norm(src, ti, dc, stslot)
                    for si_, (c0, Ls) in enumerate(segs):
                        o0 = si_ * SW + PADW
                        P.act(lambda e, o0=o0, c0=c0, Ls=Ls, n_ap=n_ap, dc=dc, r=r: e.activation(
                            out=hp[:, dc, o0:o0 + Ls], in_=n_ap[:, c0:c0 + Ls], func=AF.Identity,
                            scale=tab[:, r, 0, dc:dc + 1], bias=tab[:, r, 1, dc:dc + 1]),
                            reads=[n_b, tabb], writes=[hpb[dc // 4]])

                def windows(ti, dsl):
                    tok0, r = tiles[ti]
                    has_l = (r == 0 and ti > 0)
                    has_r = (r == 0 and ti < NT_S - 1)
                    segs = [(0, T)] if r == 0 else [(0, 256), (256, 256)]
                    L = segs[0][1]
                    SW = L + 2 * PADW
                    Wt = len(segs) * SW
                    for gi in (3, 1, 2, 0):
                        w = POOL_WINDOWS[gi]
                        es = 0 if gi in (3, 1) else 1
                        eng = P.dve if es == 0 else P.pool
                        hg = hp[:, gi * 4:(gi + 1) * 4, :]
                        cur, curb = acc[es][0], accb[es][0]
                        oth, othb = acc[es][1], accb[es][1]
                        eng(lambda e, cur=cur, hg=hg, Wt=Wt: e.tensor_tensor(out=cur[:, :, 1:Wt], in0=hg[:, :, 0:Wt - 1], in1=hg[:, :, 1:Wt], op=ALU.add),
                            reads=[hpb[gi]], writes=[curb])
                        lo, hi_, step, ww = 1, Wt, 1, 2
                        while ww < w:
                            nlo, nhi = lo + step, hi_ - step
                            eng(lambda e, oth=oth, cur=cur, nlo=nlo, nhi=nhi, step=step: e.tensor_tensor(
                                out=oth[:, :, nlo:nhi], in0=cur[:, :, nlo - step:nhi - step], in1=cur[:, :, nlo + step:nhi + step], op=ALU.add),
                                reads=[curb], writes=[othb])
                            cur, curb, oth, othb = oth, othb, cur, curb
                            lo, hi_ = nlo, nhi
                            step *= 2
                            ww *= 2
                        for si_, (c0, Ls) in enumerate(segs):
                            o0 = si_ * SW + PADW
                            if not has_l:
                                eng(lambda e, cur=cur, o0=o0, gi=gi: e.tensor_tensor(out=cur[:, :, o0:o0 + 8], in0=cur[:, :, o0:o0 + 8],
                                                                                in1=fix[:, gi, 0, :].unsqueeze(1).broadcast_to([128, 4, 8]), op=ALU.mult),
                                    reads=[curb, fixb], writes=[curb])
                            if not has_r:
                                eng(lambda e, cur=cur, o0=o0, Ls=Ls, gi=gi: e.tensor_tensor(out=cur[:, :, o0 + Ls - 8:o0 + Ls], in0=cur[:, :, o0 + Ls - 8:o0 + Ls],
                                                                                      in1=fix[:, gi, 1, :].unsqueeze(1).broadcast_to([128, 4, 8]), op=ALU.mult),
                                    reads=[curb, fixb], writes=[curb])
                            P.dve(lambda e, cur=cur, hg=hg, o0=o0, Ls=Ls, c0=c0, gi=gi, w=w, dsl=dsl: e.scalar_tensor_tensor(
                                out=d_t[dsl][:, gi * 4:(gi + 1) * 4, c0:c0 + Ls], in0=cur[:, :, o0:o0 + Ls], scalar=1.0 / w, in1=hg[:, :, o0:o0 + Ls],
                                op0=ALU.mult, op1=ALU.subtract),
                                reads=[curb, hpb[gi]], writes=[d_b[dsl]])

                def mix_dc(ti, dsl, stslot, dc):
                    gi, q = dc // 4, dc % 4
                    yb = dc % 2
                    for kc in range(4):
                        P.pe(lambda e, gi=gi, q=q, kc=kc, yb=yb, dsl=dsl: e.matmul(ps[yb][:], wp[:, gi * 4 + kc, q * 128:(q + 1) * 128], d_t[dsl][:, gi * 4 + kc, :],
                                                                      start=(kc == 0), stop=(kc == 3)),
                             reads=[wpb, d_b[dsl]], writes=[psb[yb]])
                    epilogue(src, dst, ti, dc, stslot, prev, yb, dc == 0, pe_stats=(6, 7, dc == DC - 1))

                ctx = fill_pre(0)
                for dc in range(DC):
                    fill_dc(ctx, dc)
                windows(0, 0)
                sl = {0: ctx[2]}
                for ti in range(NT):
                    nctx = fill_pre(ti + 1) if ti + 1 < NT else None
                    for dc in range(DC):
                        if nctx is not None:
                            fill_dc(nctx, dc)
                        mix_dc(ti, ti % 2, sl[ti], dc)
                    finalize(dst, ti, have_tot=True)
                    if nctx is not None:
                        sl[ti + 1] = nctx[2]
                        windows(ti + 1, (ti + 1) % 2)
                    conv_step(8)
                P.barrier()

        pass_in(0)
        if STAGE >= 1:
            pass_ffn(0, 0, 0, 0, 1, None, (mark_ffn0a, mark_att))
        if STAGE >= 2:
            conv_until(mark_att)
            pass_attn(0, 1, 0, (0, 0))
        if STAGE >= 3:
            conv_until(mark_ffn0b)
            pass_ffn(1, 0, 2, 0, 1, (0, 1), (mark_ffn0b, mark_ffn1a))
        if STAGE >= 4:
            conv_until(mark_ffn1a)
            pass_ffn(2, 1, 0, 1, 0, (0, 2), (mark_ffn1a, mark_ffn1b))
        if STAGE >= 5:
            pass_pool(1, 0, 1, (1, 0))
        if STAGE >= 6:
            conv_until(mark_ffn1b)
            pass_ffn(3, 1, 2, 1, 0, (1, 1), (mark_ffn1b, mark_ffn1b), out_ln=(1, 2))
        final_src, final_prev = {0: (0, None), 1: (1, (0, 0)), 2: (0, (0, 1)), 3: (1, (0, 2)), 4: (0, (1, 0)),
                                 5: (1, (1, 1))}.get(STAGE, (0, (1, 2)))
        if STAGE < 6:
            pass_out(final_src, final_prev)

        P.assign_tickets()
        with nc.Block() as block:
            @block.sync
            def _(e):
                P.emit("sp", e, esem)

            @block.scalar
            def _(e):
                P.emit("act", e, esem)

            @block.vector
            def _(e):
                P.emit("dve", e, esem)

            @block.gpsimd
            def _(e):
                P.emit("pool", e, esem)

            @block.tensor
            def _(e):
                P.emit("pe", e, esem)
    return nc


def _rope_tables():
    half = 32
    freqs = (np.float32(10000.0) ** (-np.arange(half, dtype=np.float32) / np.float32(half))).astype(np.float32)
    p = np.arange(128)
    col = (p % 64).astype(np.float32)
    angc = col[:, None] * freqs[None, :]
    tabR = np.zeros((32, 128, 8, 32), np.float32)
    for gb in range(32):
        row = (2 * gb + p // 64).astype(np.float32)
        angr = row[:, None] * freqs[None, :]
        tabR[gb, :, 0] = np.cos(angr); tabR[gb, :, 1] = np.cos(angr)
        tabR[gb, :, 2] = np.cos(angc); tabR[gb, :, 3] = np.cos(angc)
        tabR[gb, :, 4] = -np.sin(angr); tabR[gb, :, 5] = -np.sin(angc)
        tabR[gb, :, 6] = np.sin(angr); tabR[gb, :, 7] = np.sin(angc)
    return tabR.reshape(32, 128, 256)


_NC_CACHE = {}


def kernel(x_prompt, x_sample, cache_k, cache_v, c, c_ctx, w_mod, b_mod, ln_g, ln_b,
           ffn_w1, ffn_w3, ffn_w2, w_qkv, q_gain, k_gain, w_o, w_pool, pool_scale):
    f = lambda a: np.ascontiguousarray(np.asarray(a, dtype=np.float32))
    x_prompt, x_sample, cache_k, cache_v, c, c_ctx = map(f, (x_prompt, x_sample, cache_k, cache_v, c, c_ctx))
    tabR = _rope_tables()
    shared = {
        "w_mod": f(w_mod), "b_mod": f(b_mod).reshape(288, 128), "ln_g": f(ln_g).reshape(96, 128), "ln_b": f(ln_b).reshape(96, 128),
        "ffn_w1": f(ffn_w1).reshape(4, D, DFF), "ffn_w3": f(ffn_w3).reshape(4, D, DFF), "ffn_w2": f(ffn_w2).reshape(4, DFF, D),
        "w_qkv": f(w_qkv).reshape(D, 3072), "q_gain": f(q_gain).reshape(1, 128), "k_gain": f(k_gain).reshape(1, 128),
        "w_o": f(w_o).reshape(D, D), "w_pool": f(w_pool).reshape(4, 512, 512), "pool_scale": f(pool_scale).reshape(16, 128),
        "ropeR": tabR,
    }
    in_maps = []
    for j in range(NCORES):
        m = dict(shared)
        m["xs"] = x_sample[j]
        m["xp"] = x_prompt[4 * j:4 * j + 4].reshape(1024, D)
        m["ck"] = cache_k[j, 0].reshape(PAST, 512)
        m["cv"] = cache_v[j, 0].reshape(PAST, 512)
        m["cc"] = np.stack([c[j], c_ctx], 0).reshape(32, 128)
        in_maps.append(m)
    if "nc" not in _NC_CACHE:
        _NC_CACHE["nc"] = build_nc()
    nc = _NC_CACHE["nc"]
    res = run_bass_kernel_spmd(nc, in_maps, core_ids=list(range(NCORES)))
    y_prompt = np.empty((32, 256, D), np.float32)
    y_sample = np.empty((8, 4096, D), np.float32)
    ctx_k = np.empty((32, 1, 256, NKV, HD), np.float32)
    ctx_v = np.empty((32, 1, 256, NKV, HD), np.float32)
    y_prompt[:] = 0; y_sample[:] = 0; ctx_k[:] = 0; ctx_v[:] = 0
    for j in range(NCORES):
        r = res.results[j]
        y_sample[j] = r["ys"]
        y_prompt[4 * j:4 * j + 4] = r["yp"].reshape(4, 256, D)
        ctx_k[4 * j:4 * j + 4, 0] = r["ok"].reshape(4, 256, NKV, HD)
        ctx_v[4 * j:4 * j + 4, 0] = r["ov"].reshape(4, 256, NKV, HD)
    return (y_prompt, y_sample, ctx_k, ctx_v)
```
